# Optimizing a Trainium2 kernel written in Bass

```python
import numpy as np
import jax
import jax.numpy as jnp
from jax import lax

D_MODEL = 1024
BATCH = 8
SEQ = 2048
DEPTH = 1

GRID_W = 64
CTX_LEN = 256

NA_HEADS = 16
NA_HEAD_DIM = 64
NA_WIDTH = NA_HEADS * NA_HEAD_DIM
WIN_ROWS = 8
WIN_COLS = 16
COL_BLOCK = 16
KEY_COLS = COL_BLOCK + WIN_COLS
N_COL_BLOCKS = GRID_W // COL_BLOCK

SSM_WIDTH = 2 * D_MODEL
SSM_HEAD_DIM = 64
SSM_HEADS = SSM_WIDTH // SSM_HEAD_DIM
SSM_GROUPS = 8
HEADS_PER_GROUP = SSM_HEADS // SSM_GROUPS
SSM_STATE = 128
SSM_CONV = 5
SSM_CHUNK = 128
N_DIRS = 2
CONV_WIDTH = SSM_WIDTH + 2 * SSM_GROUPS * SSM_STATE

PROJ_SPLITS = (NA_WIDTH, NA_WIDTH, NA_WIDTH, NA_WIDTH, SSM_WIDTH, CONV_WIDTH, N_DIRS * SSM_HEADS, D_MODEL, D_MODEL)
PROJ_WIDTH = sum(PROJ_SPLITS)
EPS = 1e-6

kernel_name = 'hybrid_natten_ssd_prefix_block'


def rmsnorm(x, g):
    xf = x.astype(jnp.float32)
    y = xf * lax.rsqrt(jnp.mean(xf * xf, axis=-1, keepdims=True) + EPS)
    return (y * g.astype(jnp.float32)).astype(x.dtype)


def adaln(cond, w_mod, b_mod):
    m = jax.nn.silu(cond) @ w_mod + b_mod
    return jnp.split(m, 3, axis=-1)


def split_projection(p):
    offsets = np.cumsum(PROJ_SPLITS)[:-1].tolist()
    return jnp.split(p, offsets, axis=-1)


def to_heads(t):
    return t.reshape(*t.shape[:-1], NA_HEADS, NA_HEAD_DIM)


def neighborhood_attention(q, k, v, k_ctx, v_ctx, rpb):
    b, s, h, d = q.shape
    rows = s // GRID_W
    kh = min(WIN_ROWS, rows)
    qg = (q * (d ** -0.5)).reshape(b, rows, N_COL_BLOCKS, COL_BLOCK, h, d)
    kg = k.reshape(b, rows, GRID_W, h, d)
    vg = v.reshape(b, rows, GRID_W, h, d)
    q_cols = np.arange(GRID_W).reshape(N_COL_BLOCKS, COL_BLOCK)
    win_c0 = np.clip(q_cols - WIN_COLS // 2, 0, GRID_W - WIN_COLS)
    key_c0 = np.clip(q_cols[:, 0] - WIN_COLS // 2, 0, GRID_W - KEY_COLS)
    key_cols = key_c0[:, None] + np.arange(KEY_COLS)
    kc = key_cols[:, None, :]
    col_in_win = (kc >= win_c0[..., None]) & (kc < win_c0[..., None] + WIN_COLS)
    col_bias_idx = np.clip(kc - q_cols[..., None] + WIN_COLS - 1, 0, 2 * WIN_COLS - 2)
    n_win = kh * KEY_COLS

    def one_row(args):
        r, q_row = args
        r0 = jnp.clip(r - kh // 2, 0, rows - kh)
        k_blk = lax.dynamic_slice_in_dim(kg, r0, kh, axis=1)[:, :, key_cols]
        v_blk = lax.dynamic_slice_in_dim(vg, r0, kh, axis=1)[:, :, key_cols]
        s_win = jnp.einsum('bjqhd,bajkhd->bhjqak', q_row, k_blk).astype(jnp.float32)
        dr_idx = r0 + jnp.arange(kh) - r + WIN_ROWS - 1
        bias = rpb[:, dr_idx][:, :, col_bias_idx]
        bias = jnp.transpose(bias, (0, 2, 3, 1, 4)).astype(jnp.float32)
        s_win = jnp.where(col_in_win[:, :, None, :], s_win + bias, -jnp.inf)
        s_ctx = jnp.einsum('bjqhd,bchd->bhjqc', q_row, k_ctx).astype(jnp.float32)
        scores = jnp.concatenate([s_win.reshape(b, h, N_COL_BLOCKS, COL_BLOCK, n_win), s_ctx], axis=-1)
        p = jax.nn.softmax(scores, axis=-1).astype(v.dtype)
        p_win = p[..., :n_win].reshape(b, h, N_COL_BLOCKS, COL_BLOCK, kh, KEY_COLS)
        p_ctx = p[..., n_win:]
        o = (jnp.einsum('bhjqak,bajkhd->bjqhd', p_win, v_blk)
             + jnp.einsum('bhjqc,bchd->bjqhd', p_ctx, v_ctx))
        return o.reshape(b, GRID_W, h, d)

    out = lax.map(one_row, (jnp.arange(rows), jnp.moveaxis(qg, 1, 0)))
    return jnp.moveaxis(out, 0, 1).reshape(b, s, h, d)


def context_attention(q, k, v):
    s = jnp.einsum('bqhd,bkhd->bhqk', q * (q.shape[-1] ** -0.5), k).astype(jnp.float32)
    p = jax.nn.softmax(s, axis=-1).astype(v.dtype)
    return jnp.einsum('bhqk,bkhd->bqhd', p, v)


def depthwise_conv(u, w, bias):
    pad = w.shape[0] // 2
    out = lax.conv_general_dilated(u, w[:, None, :].astype(u.dtype), window_strides=(1,),
                                   padding=[(pad, pad)], dimension_numbers=('NWC', 'WIO', 'NWC'),
                                   feature_group_count=u.shape[-1])
    return out + bias


def split_xbc(u):
    b, l, _ = u.shape
    xs, bm, cm = jnp.split(u, [SSM_WIDTH, SSM_WIDTH + SSM_GROUPS * SSM_STATE], axis=-1)
    xs = xs.reshape(b, l, SSM_GROUPS, HEADS_PER_GROUP, SSM_HEAD_DIM)
    bm = bm.reshape(b, l, SSM_GROUPS, SSM_STATE)
    cm = cm.reshape(b, l, SSM_GROUPS, SSM_STATE)
    return xs, bm, cm


def dt_and_decay(dt_raw, a_log, dt_bias):
    dt = jax.nn.softplus(dt_raw.astype(jnp.float32) + dt_bias.astype(jnp.float32))
    a = -jnp.exp(a_log.astype(jnp.float32)) * dt
    shp = dt.shape[:-1] + (SSM_GROUPS, HEADS_PER_GROUP)
    return dt.reshape(shp), a.reshape(shp)


def to_chunks(t):
    return t.reshape(t.shape[0], t.shape[1] // SSM_CHUNK, SSM_CHUNK, *t.shape[2:])


def from_chunks(t):
    return t.reshape(t.shape[0], t.shape[1] * t.shape[2], *t.shape[3:])


def ssd_chunk_states(xdt, a, bm, h0):
    a_cum = jnp.cumsum(a, axis=2)
    a_tot = a_cum[:, :, -1]
    states = jnp.einsum('bcqgn,bcqgr,bcqgrp->bcgrpn', bm, jnp.exp(a_tot[:, :, None] - a_cum), xdt)

    def step(h, inp):
        s_c, a_c = inp
        return jnp.exp(a_c)[..., None, None] * h + s_c, h

    h_final, h_start = lax.scan(step, h0, (jnp.moveaxis(states, 1, 0), jnp.moveaxis(a_tot, 1, 0)))
    return a_cum, jnp.moveaxis(h_start, 0, 1), h_final


def ssd_scan(xdt, a, bm, cm, h0):
    a_cum, h_start, h_final = ssd_chunk_states(xdt, a, bm, h0)
    q = a_cum.shape[2]
    tri = jnp.tril(jnp.ones((q, q), dtype=bool))[:, :, None, None]
    seg = a_cum[:, :, :, None] - a_cum[:, :, None, :]
    decay = jnp.exp(jnp.where(tri, seg, -jnp.inf))
    cb = jnp.einsum('bcign,bcjgn->bcijg', cm, bm)
    y_diag = jnp.einsum('bcijgr,bcjgrp->bcigrp', cb[..., None] * decay, xdt)
    y_off = jnp.einsum('bcign,bcgrpn->bcigrp', cm, h_start) * jnp.exp(a_cum)[..., None]
    return y_diag + y_off, h_final


def ssd_mixer(u_lat, dt_lat, u_ctx, dt_ctx, a_log, dt_bias, d_skip, ctx_out):
    xs_l, bm_l, cm_l = split_xbc(u_lat)
    xs_c, bm_c, cm_c = split_xbc(u_ctx)
    skip = d_skip.reshape(SSM_GROUPS, HEADS_PER_GROUP, 1)
    h_zero = jnp.zeros((u_lat.shape[0], SSM_GROUPS, HEADS_PER_GROUP, SSM_HEAD_DIM, SSM_STATE), jnp.float32)
    y_l = xs_l * skip
    y_c = xs_c * skip if ctx_out else None
    for d in range(N_DIRS):
        orient = (lambda t: jnp.flip(t, axis=1)) if d == 1 else (lambda t: t)
        hs = slice(d * SSM_HEADS, (d + 1) * SSM_HEADS)
        dtc, ac = dt_and_decay(dt_ctx[..., hs], a_log[d], dt_bias[d])
        xc = to_chunks(orient(xs_c * dtc[..., None]))
        ac = to_chunks(orient(ac))
        bc = to_chunks(orient(bm_c))
        if ctx_out:
            yc, h_ctx = ssd_scan(xc, ac, bc, to_chunks(orient(cm_c)), h_zero)
            y_c = y_c + orient(from_chunks(yc))
        else:
            h_ctx = ssd_chunk_states(xc, ac, bc, h_zero)[2]
        dtl, al = dt_and_decay(dt_lat[..., hs], a_log[d], dt_bias[d])
        yl, _ = ssd_scan(to_chunks(orient(xs_l * dtl[..., None])), to_chunks(orient(al)),
                         to_chunks(orient(bm_l)), to_chunks(orient(cm_l)), h_ctx)
        y_l = y_l + orient(from_chunks(yl))
    y_l = y_l.reshape(*u_lat.shape[:2], SSM_WIDTH)
    if ctx_out:
        y_c = y_c.reshape(*u_ctx.shape[:2], SSM_WIDTH)
    return y_l, y_c


def gated_group_rmsnorm(y, z, g):
    b, l, w = y.shape
    u = (y * jax.nn.silu(z)).astype(jnp.float32).reshape(b, l, SSM_GROUPS, w // SSM_GROUPS)
    u = u * lax.rsqrt(jnp.mean(u * u, axis=-1, keepdims=True) + EPS)
    return u.reshape(b, l, w) * g.astype(jnp.float32)


def merge_branches(y_na, y_ssm, g_na, g_ssm, w_na_out, w_ssm_out, w_out):
    m = (jax.nn.sigmoid(g_na) * (y_na @ w_na_out)
         + jax.nn.sigmoid(g_ssm) * (y_ssm.astype(y_na.dtype) @ w_ssm_out))
    return m @ w_out


def hybrid_layer(x, cx, c, c_ctx, w_mod, b_mod, g_pre, g_post, w_in, conv_w, conv_b, a_log, dt_bias,
                 d_skip, ssm_norm_g, rpb, w_na_out, w_ssm_out, w_out, update_ctx):
    b, s, _ = x.shape
    shift_l, scale_l, gate_l = adaln(c, w_mod, b_mod)
    shift_c, scale_c, gate_c = adaln(c_ctx, w_mod, b_mod)
    h_l = rmsnorm(x, g_pre) * (1 + scale_l[:, None]) + shift_l[:, None]
    h_c = rmsnorm(cx, g_pre) * (1 + scale_c) + shift_c
    q_l, k_l, v_l, zna_l, zssm_l, xbc_l, dt_l, gna_l, gssm_l = split_projection(h_l @ w_in)
    q_c, k_c, v_c, zna_c, zssm_c, xbc_c, dt_c, gna_c, gssm_c = split_projection(h_c @ w_in)
    o_na = neighborhood_attention(to_heads(q_l), to_heads(k_l), to_heads(v_l), to_heads(k_c), to_heads(v_c), rpb)
    y_na = o_na.reshape(b, s, NA_WIDTH) * jax.nn.silu(zna_l)
    u_l = jax.nn.silu(depthwise_conv(xbc_l, conv_w, conv_b))
    u_c = jax.nn.silu(depthwise_conv(xbc_c, conv_w, conv_b))
    ys_l, ys_c = ssd_mixer(u_l, dt_l, u_c, dt_c, a_log, dt_bias, d_skip, update_ctx)
    y_ssm = gated_group_rmsnorm(ys_l, zssm_l, ssm_norm_g)
    merged = merge_branches(y_na, y_ssm, gna_l, gssm_l, w_na_out, w_ssm_out, w_out)
    x_new = (x + gate_l[:, None] * rmsnorm(merged, g_post)).astype(x.dtype)
    if not update_ctx:
        return x_new, cx
    o_c = context_attention(to_heads(q_c), to_heads(k_c), to_heads(v_c))
    y_na_c = o_c.reshape(*cx.shape[:2], NA_WIDTH) * jax.nn.silu(zna_c)
    y_ssm_c = gated_group_rmsnorm(ys_c, zssm_c, ssm_norm_g)
    merged_c = merge_branches(y_na_c, y_ssm_c, gna_c, gssm_c, w_na_out, w_ssm_out, w_out)
    cx_new = (cx + gate_c * rmsnorm(merged_c, g_post)).astype(cx.dtype)
    return x_new, cx_new


def setup_inputs(seed: int = 0) -> dict:
    key = jax.random.key(seed)
    ks = jax.random.split(key, 20)
    f32 = jnp.float32

    def nrm(k, shape, scale):
        return jax.random.normal(k, shape, f32) * scale

    dt0 = jnp.exp(jax.random.uniform(ks[12], (DEPTH, N_DIRS, SSM_HEADS), f32,
                                     float(np.log(1e-3)), float(np.log(1e-1))))
    return {
        'x': nrm(ks[0], (BATCH, SEQ, D_MODEL), 1.0),
        'c': nrm(ks[1], (BATCH, D_MODEL), 1.0),
        'ctx': nrm(ks[2], (BATCH, CTX_LEN, D_MODEL), 1.0),
        'c_ctx': nrm(ks[3], (D_MODEL,), 1.0),
        'w_mod': nrm(ks[4], (DEPTH, D_MODEL, 3 * D_MODEL), 0.5 * D_MODEL ** -0.5),
        'b_mod': nrm(ks[5], (DEPTH, 3 * D_MODEL), 0.01),
        'g_pre': 1.0 + nrm(ks[6], (DEPTH, D_MODEL), 0.01),
        'g_post': 1.0 + nrm(ks[7], (DEPTH, D_MODEL), 0.01),
        'w_in': nrm(ks[8], (DEPTH, D_MODEL, PROJ_WIDTH), D_MODEL ** -0.5),
        'conv_w': nrm(ks[9], (DEPTH, SSM_CONV, CONV_WIDTH), SSM_CONV ** -0.5),
        'conv_b': nrm(ks[10], (DEPTH, CONV_WIDTH), 0.01),
        'a_log': jnp.log(jax.random.uniform(ks[11], (DEPTH, N_DIRS, SSM_HEADS), f32, 1.0, 16.0)),
        'dt_bias': dt0 + jnp.log(-jnp.expm1(-dt0)),
        'd_skip': 1.0 + nrm(ks[13], (DEPTH, SSM_HEADS), 0.01),
        'ssm_norm_g': 1.0 + nrm(ks[14], (DEPTH, SSM_WIDTH), 0.01),
        'rpb': nrm(ks[15], (DEPTH, NA_HEADS, 2 * WIN_ROWS - 1, 2 * WIN_COLS - 1), 0.02),
        'w_na_out': nrm(ks[16], (DEPTH, NA_WIDTH, D_MODEL), NA_WIDTH ** -0.5),
        'w_ssm_out': nrm(ks[17], (DEPTH, SSM_WIDTH, D_MODEL), SSM_WIDTH ** -0.5),
        'w_out': nrm(ks[18], (DEPTH, D_MODEL, D_MODEL), D_MODEL ** -0.5),
    }


def reference(x, c, ctx, c_ctx, w_mod, b_mod, g_pre, g_post, w_in, conv_w, conv_b, a_log, dt_bias,
              d_skip, ssm_norm_g, rpb, w_na_out, w_ssm_out, w_out):
    cx = ctx
    for layer in range(DEPTH):
        x, cx = hybrid_layer(x, cx, c, c_ctx, w_mod[layer], b_mod[layer], g_pre[layer], g_post[layer],
                             w_in[layer], conv_w[layer], conv_b[layer], a_log[layer], dt_bias[layer],
                             d_skip[layer], ssm_norm_g[layer], rpb[layer], w_na_out[layer],
                             w_ssm_out[layer], w_out[layer], update_ctx=layer < DEPTH - 1)
    return x
```

```python
import numpy as np
import ml_dtypes
import concourse.bass as bass
import concourse.mybir as mybir
from concourse.bass_utils import run_bass_kernel_spmd
from concourse.alu_op_type import AluOpType as ALU

F32 = mybir.dt.float32
BF16 = mybir.dt.bfloat16
AF = mybir.ActivationFunctionType

D = 1024
S = 2048
CL = 256
NT = 16
NTT = 18
PW = 12352
NEG = -30000.0
EPS = 1e-6
OQ, OK_, OV, OZNA, OZS, OXS, OB, OC, ODT, OGNA, OGS = 0, 1024, 2048, 3072, 4096, 6144, 8192, 9216, 10240, 10304, 11328


def I(m, *a, **kw):
    return (m, a, kw)


class Buf:
    __slots__ = ("name", "writers", "readers")

    def __init__(self, name):
        self.name = name
        self.writers = []
        self.readers = []


class Prog:
    ENGS = ("pe", "act", "dve", "pool", "sp")

    def __init__(self, nc):
        self.nc = nc
        self.q = {e: [] for e in self.ENGS}
        self.sem = {e: nc.alloc_semaphore("sem_" + e) for e in self.ENGS}
        self.cnt = {e: 0 for e in self.ENGS}
        self.waited = {e: {} for e in self.ENGS}
        self.dsem = {}
        self.semh = dict(self.sem)
        self.last = {}

    def _need(self, eng, toks):
        best = {}
        for (k, v) in toks:
            if k == eng and eng == "pe":
                continue
            if best.get(k, 0) < v:
                best[k] = v
        for k, v in best.items():
            if self.waited[eng].get(k, 0) >= v:
                continue
            self.waited[eng][k] = v
            h = self.semh[k]
            self.q[eng].append(lambda e, h=h, v=v: e.wait_ge(h, v))

    def op(self, eng, fn, reads=(), writes=()):
        toks = []
        for b in reads:
            toks += b.writers
        for b in writes:
            toks += [t for t in b.writers if t[0] != eng]
            toks += [t for t in b.readers if t[0] != eng]
        self._need(eng, toks)
        self.cnt[eng] += 1
        tok = (eng, self.cnt[eng])
        self.last[eng] = tok
        h = self.sem[eng]
        self.q[eng].append(lambda e, fn=fn, h=h: getattr(e, fn[0])(*fn[1], **fn[2]).then_inc(h, 1))
        for b in reads:
            b.readers.append(tok)
        for b in writes:
            b.writers = [tok]
            b.readers = []
        return tok

    def mm(self, fns, reads=(), writes=()):
        toks = []
        for b in reads:
            toks += b.writers
        for b in writes:
            toks += b.writers
            toks += b.readers
        self._need("pe", toks)
        self.cnt["pe"] += 1
        tok = ("pe", self.cnt["pe"])
        self.last["pe"] = tok
        h = self.sem["pe"]
        n = len(fns)
        for i, fn in enumerate(fns):
            if i == n - 1:
                self.q["pe"].append(lambda e, fn=fn, h=h: getattr(e, fn[0])(*fn[1], **fn[2]).then_inc(h, 1))
            else:
                self.q["pe"].append(lambda e, fn=fn: getattr(e, fn[0])(*fn[1], **fn[2]))
        for b in reads:
            b.readers.append(tok)
        for b in writes:
            b.writers = [tok]
            b.readers = []
        return tok

    def dma(self, eng, semname, out, in_, reads=(), writes=(), **kw):
        if semname not in self.dsem:
            h = self.nc.alloc_semaphore("dsem_" + semname)
            self.dsem[semname] = [h, 0]
            self.semh["d:" + semname] = h
        ent = self.dsem[semname]
        toks = []
        for b in reads:
            toks += b.writers
        for b in writes:
            toks += b.writers
            toks += b.readers
        self._need(eng, toks)
        ent[1] += 16
        tok = ("d:" + semname, ent[1])
        self.last["d:" + semname] = tok
        h = ent[0]
        self.q[eng].append(
            lambda e, out=out, in_=in_, h=h, kw=kw: e.dma_start(out=out, in_=in_, **kw).then_inc(h, 16))
        for b in reads:
            b.readers.append(tok)
        for b in writes:
            b.writers = [tok]
            b.readers = []
        return tok

    def barrier(self, engs=None):
        toks = list(self.last.values())
        for e in (engs or self.ENGS):
            self._need(e, toks)

    def emit(self):
        nc = self.nc
        with nc.Block() as block:
            @block.sync
            def _(e):
                for f in self.q["sp"]:
                    f(e)

            @block.tensor
            def _(e):
                for f in self.q["pe"]:
                    f(e)

            @block.scalar
            def _(e):
                for f in self.q["act"]:
                    f(e)

            @block.vector
            def _(e):
                for f in self.q["dve"]:
                    f(e)

            @block.gpsimd
            def _(e):
                for f in self.q["pool"]:
                    f(e)


def _consts():
    c = {}
    t = np.arange(128)
    c["ident"] = np.eye(128, dtype=np.float32)
    c["identb"] = np.eye(128, dtype=np.float32).astype(ml_dtypes.bfloat16)
    c["negib"] = (np.eye(128, dtype=np.float32) * NEG).astype(ml_dtypes.bfloat16)
    c["ones"] = np.ones((128, 128), np.float32)
    c["onesb"] = np.ones((128, 128), np.float32).astype(ml_dtypes.bfloat16)
    c["u_f"] = (t[:, None] <= t[None, :]).astype(np.float32)
    c["u_b"] = (t[:, None] >= t[None, :]).astype(np.float32)
    c["ls_f"] = (t[:, None] > t[None, :]).astype(np.float32)
    c["ls_b"] = (t[:, None] < t[None, :]).astype(np.float32)
    for n_ in ("u_f", "u_b", "ls_f", "ls_b"):
        c[n_.replace("u_", "ub_").replace("ls_", "lsb_")] = c[n_].astype(ml_dtypes.bfloat16)
    mf = (t[None, :] < t[:, None]).astype(np.float32)
    mb = (t[None, :] > t[:, None]).astype(np.float32)
    c["mi_f"] = np.tile(mf[:, None, :], (1, 4, 1)).reshape(128, 512).astype(ml_dtypes.bfloat16)
    c["mi_b"] = np.tile(mb[:, None, :], (1, 4, 1)).reshape(128, 512).astype(ml_dtypes.bfloat16)
    kc = np.arange(64)[:, None]
    qc = np.arange(64)[None, :]
    w0 = np.clip(qc - 8, 0, 48)
    colok = (kc >= w0) & (kc < w0 + 16)
    mI = np.zeros((2, 64, 16, 64), np.float32)
    mE = np.zeros((2, 64, 16, 64), np.float32)
    for krl in range(2):
        for m in range(16):
            dr = 7 - m + krl
            okI = colok & (dr >= -4) & (dr <= 3)
            okE = colok & (dr >= -7) & (dr <= 7)
            mI[krl, :, m, :] = np.where(okI, 0.0, NEG)
            mE[krl, :, m, :] = np.where(okE, 0.0, NEG)
    c["mask_i"] = mI.reshape(128, 1024).astype(ml_dtypes.bfloat16)
    c["mask_e"] = mE.reshape(128, 1024).astype(ml_dtypes.bfloat16)
    return c


def _rpb_layout(rpb):
    kc = np.arange(64)[:, None]
    qc = np.arange(64)[None, :]
    ci = np.clip(kc - qc + 15, 0, 30)
    out = np.empty((16, 2, 64, 16, 64), np.float32)
    for krl in range(2):
        for m in range(16):
            ri = int(np.clip(14 - m + krl, 0, 14))
            out[:, krl, :, m, :] = rpb[:, ri][:, ci]
    return np.ascontiguousarray(out.reshape(16, 128, 1024))


CONST_SPECS = [("ident", [128, 128], F32), ("identb", [128, 128], BF16), ("negib", [128, 128], BF16),
               ("ones", [128, 128], F32), ("onesb", [128, 128], BF16), ("u_f", [128, 128], F32), ("u_b", [128, 128], F32),
               ("ls_f", [128, 128], F32), ("ls_b", [128, 128], F32),
               ("ub_f", [128, 128], BF16), ("ub_b", [128, 128], BF16), ("lsb_f", [128, 128], BF16), ("lsb_b", [128, 128], BF16), ("mi_f", [128, 512], BF16),
               ("mi_b", [128, 512], BF16), ("mask_i", [128, 1024], BF16), ("mask_e", [128, 1024], BF16)]


class Arena:
    def __init__(self, nc, lo=16384, hi=196608):
        self.nc, self.top, self.hi = nc, lo, hi
        self.peak = lo

    def alloc(self, name, shape, dt):
        n = 1
        for d_ in shape[1:]:
            n *= d_
        nbytes = n * (4 if dt == F32 else 2)
        off = (self.top + 31) // 32 * 32
        self.top = off + nbytes
        self.peak = max(self.peak, self.top)
        assert self.top <= self.hi, "SBUF arena overflow at %s: %d" % (name, self.top)
        return self.nc.alloc_sbuf_tensor_at("s_" + name, list(shape), dt, offset=off)


def build(stage=99, dbg=(), skip=(), groups=8):
    nc = bass.Bass("TRN2", target_bir_lowering=False)
    P = Prog(nc)
    A = Arena(nc)
    din = {}
    debug = len(dbg) > 0

    def dram_in(name, shape, dt=F32):
        din[name] = nc.dram_tensor(name, list(shape), dt, kind="ExternalInput").ap()
        return din[name]

    x_d = dram_in("x", [S, D])
    ctx_d = dram_in("ctx", [CL, D])
    cvT_d = dram_in("cvT", [128, 8, 2])
    wmod_d = dram_in("w_mod", [D, 3 * D])
    bmod_d = dram_in("b_mod", [1, 3 * D])
    gpreT_d = dram_in("g_preT", [128, 8])
    gpost_d = dram_in("g_post", [1, D])
    win_d = dram_in("w_in", [D, PW])
    cwT_d = dram_in("conv_wT", [128, 32, 5])
    cbT_d = dram_in("conv_bT", [128, 32])
    cbrow_d = dram_in("conv_brow", [1, 4096])
    alog_d = dram_in("a_log", [1, 64])
    dtb_d = dram_in("dt_bias", [1, 64])
    dsk_d = dram_in("d_skip", [1, 32])
    sng_d = dram_in("ssm_norm_g", [1, 2048])
    rpb_d = dram_in("rpbT", [16, 128, 1024])
    wna_d = dram_in("w_na_out", [D, D])
    wss_d = dram_in("w_ssm_out", [2 * D, D])
    wout_d = dram_in("w_out", [D, D])
    cst_d = {n: dram_in("c_" + n, shp, dt) for (n, shp, dt) in CONST_SPECS}
    out_d = nc.dram_tensor("out", [S, D], F32, kind="ExternalOutput").ap()
    skind = "ExternalOutput" if debug else "Internal"
    ynaT_d = nc.dram_tensor("ynaT_scr", [D, S], BF16, kind=skind).ap()
    yssT_d = nc.dram_tensor("yssT_scr", [2 * D, S], BF16, kind=skind).ap()
    sg_d = nc.dram_tensor("sg_scr", [2 * D, S], F32, kind=skind).ap()
    dbg_d = {}
    win_v = win_d.rearrange("(kc p) n -> p kc n", p=128)

    psall = nc.alloc_psum_tensor("psall", [128, 4096], F32)
    psb = [Buf("ps%d" % i) for i in range(8)]

    def PS(b, c0=0, c1=512):
        return psall[:, b * 512 + c0: b * 512 + c1]

    cst = {}
    cstb = Buf("consts")
    for (n, shp, dt) in CONST_SPECS:
        if n.startswith("mask_"):
            continue
        cst[n] = A.alloc("k_" + n, shp, dt)
        P.dma("sp", "consts", cst[n][:], cst_d[n], writes=[cstb])

    def dbg_out(name, ap, shape, dt, bufs):
        if name in dbg:
            dbg_d[name] = nc.dram_tensor("dbg_" + name, list(shape), dt, kind="ExternalOutput").ap()
            P.dma("sp", "dbg_" + name, dbg_d[name], ap, reads=bufs)

    gs = A.alloc("gs", [128, 2, 8], F32)
    sh = A.alloc("sh", [128, 2, 8], F32)
    ggp = A.alloc("ggp", [128, D], F32)
    modb = Buf("mod")
    mark_h = A.top
    hT = A.alloc("hT", [128, 8, S], BF16)
    hTc = A.alloc("hTc", [128, 8, CL], BF16)
    hTb = Buf("hT")
    mark0 = A.top

    def new_phase():
        P.barrier()
        A.top = mark0

    cvT = A.alloc("cvT", [128, 8, 2], F32)
    sc = A.alloc("sc", [128, 8, 2], F32)
    lhsA = A.alloc("lhsA", [128, 8, 128], F32)
    lhsB = A.alloc("lhsB", [128, 8, 128], F32)
    bm = A.alloc("bm", [1, 3 * D], F32)
    gpreT = A.alloc("gpreT", [128, 8], F32)
    gpost_bc = A.alloc("gpost_bc", [128, D], F32)
    wm = [A.alloc("wm%d" % i, [128, 3 * D], F32) for i in range(2)]
    wmb = [Buf("wm%d" % i) for i in range(2)]
    mrow = A.alloc("mrow", [128, 2048], F32)
    mT = A.alloc("mT", [128, 32], F32)
    ab = Buf("adaln_small")
    lb = Buf("lhsAB")
    P.dma("sp", "adaln_in", cvT[:], cvT_d, writes=[ab])
    P.dma("sp", "adaln_in", bm[:], bmod_d, writes=[ab])
    P.dma("sp", "adaln_in", gpreT[:], gpreT_d, writes=[ab])
    P.dma("sp", "adaln_in", gpost_bc[:], gpost_d.partition_broadcast(128), writes=[ab])
    P.op("act", I("activation", out=sc[:], in_=cvT[:], func=AF.Silu), reads=[ab], writes=[lb])
    P.op("dve", I("tensor_copy", out=lhsA[:, :, 0:64], in_=sc[:, :, 0:1].to_broadcast([128, 8, 64])),
         reads=[lb], writes=[lb])
    P.op("dve", I("tensor_copy", out=lhsA[:, :, 64:128], in_=sc[:, :, 1:2].to_broadcast([128, 8, 64])),
         reads=[lb], writes=[lb])
    P.op("dve", I("tensor_copy", out=lhsB[:], in_=sc[:, :, 0:1].to_broadcast([128, 8, 128])),
         reads=[lb], writes=[lb])
    for k in range(8):
        s_ = k % 2
        P.dma("sp", "wm%d" % s_, wm[s_][:], wmod_d[k * 128:(k + 1) * 128, :], writes=[wmb[s_]])
        for n in range(6):
            lh = lhsA if n < 4 else lhsB
            P.mm([I("matmul", PS(n), lhsT=lh[:, k, :],
                                                              rhs=wm[s_][:, n * 512:(n + 1) * 512],
                                                              start=(k == 0), stop=False)],
                 reads=[lb, wmb[s_]], writes=[psb[n]])
    for n in range(6):
        P.mm([I("matmul", PS(n), lhsT=cst["ones"][0:1, :], rhs=bm[0:1, n * 512:(n + 1) * 512],
                                      start=False, stop=True)],
             reads=[ab, cstb], writes=[psb[n]])
    mrb = Buf("mrow")
    for n in range(4):
        P.op("act", I("activation", out=mrow[:, n * 512:(n + 1) * 512], in_=PS(n), func=AF.Copy),
             reads=[psb[n]], writes=[mrb])
    for n in range(2):
        P.op("dve", I("tensor_tensor", out=ggp[:, n * 512:(n + 1) * 512], in0=PS(4 + n),
                                                   in1=gpost_bc[:, n * 512:(n + 1) * 512], op=ALU.mult),
             reads=[psb[4 + n], ab], writes=[modb])
    fns = []
    for lc in range(2):
        p0 = 64 * lc
        for v in range(2):
            for ch in range(8):
                idx = (lc * 2 + v) * 8 + ch
                fns.append(I("matmul",
                    PS(6, idx, idx + 1), lhsT=mrow[p0:p0 + 1, v * 1024 + ch * 128: v * 1024 + (ch + 1) * 128],
                    rhs=cst["ones"][p0:p0 + 1, 0:1], start=True, stop=True))
    P.mm(fns, reads=[mrb, cstb], writes=[psb[6]])
    P.op("dve", I("tensor_copy", out=mT[:], in_=PS(6, 0, 32)), reads=[psb[6]], writes=[mrb])
    for lc in range(2):
        P.op("dve", I("tensor_copy", out=sh[:, lc, :], in_=mT[:, lc * 16: lc * 16 + 8]),
             reads=[mrb], writes=[modb])
        P.op("dve", I("scalar_tensor_tensor", out=gs[:, lc, :], in0=mT[:, lc * 16 + 8: lc * 16 + 16],
                                                            scalar=1.0, in1=gpreT[:], op0=ALU.add, op1=ALU.mult),
             reads=[mrb, ab], writes=[modb])
    dbg_out("gs", gs[:], [128, 2, 8], F32, [modb])
    dbg_out("sh", sh[:], [128, 2, 8], F32, [modb])
    dbg_out("ggp", ggp[:], [128, D], F32, [modb])

    xt = [A.alloc("xt%d" % i, [128, D], F32) for i in range(2)]
    xtb = [Buf("xt%d" % i) for i in range(2)]
    junk = A.alloc("junk", [128, D], BF16)
    junkb = Buf("junk")
    ss = [A.alloc("ss%d" % i, [128, 4], F32) for i in range(2)]
    ssb = [Buf("ss%d" % i) for i in range(2)]
    dg = [A.alloc("dg%d" % i, [128, 128], F32) for i in range(2)]
    dgb = [Buf("dg%d" % i) for i in range(2)]
    for it in range(NTT):
        s_ = it % 2
        lc = 0 if it < NT else 1
        src = x_d[it * 128:(it + 1) * 128, :] if it < NT else ctx_d[(it - NT) * 128:(it - NT + 1) * 128, :]
        P.dma("sp", "xt%d" % s_, xt[s_][:], src, writes=[xtb[s_]])
        P.op("act", I("activation", out=junk[:], in_=xt[s_][:], func=AF.Square,
                                                  accum_out=ss[s_][:, 0:1]),
             reads=[xtb[s_]], writes=[junkb, ssb[s_]])
        P.op("act", I("activation", out=ss[s_][:, 1:2], in_=ss[s_][:, 0:1], func=AF.Sqrt,
                                                  scale=1.0 / D, bias=EPS),
             reads=[ssb[s_]], writes=[ssb[s_]])
        P.op("dve", I("reciprocal", out=ss[s_][:, 2:3], in_=ss[s_][:, 1:2]),
             reads=[ssb[s_]], writes=[ssb[s_]])
        P.op("dve", I("tensor_scalar", out=dg[s_][:], in0=cst["ident"][:], scalar1=ss[s_][:, 2:3],
                                                     scalar2=None, op0=ALU.mult),
             reads=[ssb[s_], cstb], writes=[dgb[s_]])
        b0 = 2 * s_
        for hb in range(2):
            P.mm([I("matmul",
                PS(b0 + hb, j * 128, (j + 1) * 128), lhsT=xt[s_][:, (hb * 4 + j) * 128:(hb * 4 + j + 1) * 128],
                rhs=dg[s_][:], start=True, stop=True) for j in range(4)],
                reads=[xtb[s_], dgb[s_]], writes=[psb[b0 + hb]])
        for ch in range(8):
            dst = hT[:, ch, it * 128:(it + 1) * 128] if it < NT else hTc[:, ch, (it - NT) * 128:(it - NT + 1) * 128]
            P.op("act", I("activation",
                out=dst, in_=PS(b0 + ch // 4, (ch % 4) * 128, (ch % 4 + 1) * 128), func=AF.Identity,
                scale=gs[:, lc, ch:ch + 1], bias=sh[:, lc, ch:ch + 1]),
                reads=[psb[b0 + ch // 4], modb], writes=[hTb])
    dbg_out("hT", hT[:], [128, 8, S], BF16, [hTb])
    dbg_out("hTc", hTc[:], [128, 8, CL], BF16, [hTb])

    evac_rr = [0]

    def evac_copy(out_ap, in_ap, reads, writes, scale=None):
        evac_rr[0] += 1
        if scale is not None or evac_rr[0] % 2 == 0:
            if scale is None:
                P.op("act", I("activation", out=out_ap, in_=in_ap, func=AF.Copy), reads=reads, writes=writes)
            else:
                P.op("act", I("activation", out=out_ap, in_=in_ap, func=AF.Copy, scale=scale),
                     reads=reads, writes=writes)
        else:
            P.op("dve", I("tensor_copy", out=out_ap, in_=in_ap), reads=reads, writes=writes)

    def proj_fm(w_ap, wbuf, bank, tb, ctx=False):
        if ctx:
            P.mm([I("matmul", PS(bank, 0, CL), lhsT=w_ap[:, k, :], rhs=hTc[:, k, :],
                                          start=(k == 0), stop=(k == 7)) for k in range(8)],
                 reads=[wbuf, hTb], writes=[psb[bank]])
        else:
            P.mm([I("matmul", PS(bank), lhsT=w_ap[:, k, :], rhs=hT[:, k, tb * 512:(tb + 1) * 512],
                                          start=(k == 0), stop=(k == 7)) for k in range(8)],
                 reads=[wbuf, hTb], writes=[psb[bank]])

    def htile(k, T):
        return hT[:, k, T * 128:(T + 1) * 128] if T < NT else hTc[:, k, (T - NT) * 128:(T - NT + 1) * 128]

    if stage >= 2 and 'att' not in skip:
        new_phase()
        wq = [A.alloc("wq%d" % i, [128, 8, 128], BF16) for i in range(2)]
        wk = [A.alloc("wk%d" % i, [128, 8, 128], BF16) for i in range(2)]
        wv = [A.alloc("wv%d" % i, [128, 8, 128], BF16) for i in range(2)]
        wz = [A.alloc("wz%d" % i, [128, 8, 128], BF16) for i in range(2)]
        wab = [[Buf("wa%d_%d" % (j, i)) for i in range(2)] for j in range(4)]
        qT = A.alloc("qT", [128, S], BF16)
        kT = A.alloc("kT", [128, S + CL], BF16)
        vaug = A.alloc("vaug", [128, NTT, 2, 128], BF16)
        szT = A.alloc("szT", [128, S], BF16)
        ynaT = [A.alloc("ynaT%d" % i, [128, S], BF16) for i in range(2)]
        rp = A.alloc("rp", [128, 2, 1024], BF16)
        bti = A.alloc("bti", [128, 2, 1024], BF16)
        bte = A.alloc("bte", [128, 2, 1024], BF16)
        pt = [A.alloc("pt%d" % i, [128, 1024], BF16) for i in range(2)]
        rc = [A.alloc("rc%d" % i, [128, 256], F32) for i in range(2)]
        tt = [A.alloc("tt%d" % i, [128, 256], F32) for i in range(2)]
        qTb, kTb, vb, szb, rpb_, btb = Buf("qT"), Buf("kT"), Buf("vaug"), Buf("szT"), Buf("rp"), Buf("bt")
        mkb = Buf("masks")
        for n in ("mask_i", "mask_e"):
            cst[n] = A.alloc("k_" + n, [128, 1024], BF16)
            P.dma("sp", "masks", cst[n][:], cst_d[n], writes=[mkb])
        ynb = [Buf("ynaT%d" % i) for i in range(2)]
        ptb = [Buf("pt%d" % i) for i in range(2)]
        rcb = [Buf("rc%d" % i) for i in range(2)]
        P.op("pool", I("memset", vaug[:, :, 0, 64:128], 1.0), writes=[vb])
        P.op("pool", I("memset", vaug[:, :, 1, 0:64], 1.0), writes=[vb])
        cnt_s = 0
        cnt_o = 0
        for hp in range(8):
            par = hp % 2
            for j, (w, col) in enumerate(((wq, OQ), (wk, OK_), (wv, OV), (wz, OZNA))):
                P.dma("pool", "wa%d_%d" % (j, par), w[par][:], win_v[:, :, col + hp * 128: col + (hp + 1) * 128],
                      writes=[wab[j][par]])
            for hh in range(2):
                P.dma("pool", "rp", rp[:, hh, :], rpb_d[2 * hp + hh], writes=[rpb_])
            P.op("dve", I("tensor_tensor", out=bti[:], in0=rp[:],
                                                  in1=cst["mask_i"][:].unsqueeze(1).to_broadcast([128, 2, 1024]),
                                                  op=ALU.add), reads=[rpb_, mkb], writes=[btb])
            P.op("dve", I("tensor_tensor", out=bte[:], in0=rp[:],
                                                  in1=cst["mask_e"][:].unsqueeze(1).to_broadcast([128, 2, 1024]),
                                                  op=ALU.add), reads=[rpb_, mkb], writes=[btb])
            for tb in range(4):
                bank = 6 + tb % 2
                proj_fm(wq[par], wab[0][par], bank, tb)
                evac_copy(qT[:, tb * 512:(tb + 1) * 512], PS(bank), [psb[bank]], [qTb], scale=0.125)
            for tb in range(4):
                bank = 6 + tb % 2
                proj_fm(wk[par], wab[1][par], bank, tb)
                evac_copy(kT[:, tb * 512:(tb + 1) * 512], PS(bank), [psb[bank]], [kTb])
            proj_fm(wk[par], wab[1][par], 6, 0, ctx=True)
            evac_copy(kT[:, S:S + CL], PS(6, 0, CL), [psb[6]], [kTb])
            for tb in range(4):
                bank = 6 + (tb + 1) % 2
                proj_fm(wz[par], wab[3][par], bank, tb)
                P.op("act", I("activation", out=szT[:, tb * 512:(tb + 1) * 512], in_=PS(bank),
                                                                     func=AF.Silu), reads=[psb[bank]], writes=[szb])
            for t0 in range(0, NTT, 4):
                n = min(4, NTT - t0)
                bank = 6 + (t0 // 4) % 2
                for i in range(n):
                    T = t0 + i
                    P.mm([I("matmul", PS(bank, i * 128, (i + 1) * 128), lhsT=htile(k, T),
                                                                       rhs=wv[par][:, k, :], start=(k == 0), stop=(k == 7))
                          for k in range(8)], reads=[wab[2][par], hTb], writes=[psb[bank]])
                src = PS(bank, 0, n * 128).rearrange("p (t c) -> p t c", c=128)
                P.op("act", I("activation", out=vaug[:, t0:t0 + n, 0, 0:64], in_=src[:, :, 0:64],
                                                                        func=AF.Copy), reads=[psb[bank]], writes=[vb])
                P.op("dve", I("tensor_copy", out=vaug[:, t0:t0 + n, 1, 64:128],
                                                                         in_=src[:, :, 64:128]), reads=[psb[bank]], writes=[vb])
            if hp == 0:
                dbg_out("qT", qT[:], [128, S], BF16, [qTb])
                dbg_out("kT", kT[:], [128, S + CL], BF16, [kTb])
                dbg_out("vaug", vaug[:], [128, NTT, 2, 128], BF16, [vb])
                dbg_out("bti", bti[:], [128, 2, 1024], BF16, [btb])
            for hh in range(2):
                r0 = 64 * hh
                o0 = 64 * (1 - hh)
                for b in range(8):
                    if b == 0:
                        wt = [(T, bte, 7 - 2 * T) for T in range(4)]
                    elif b == 7:
                        wt = [(12 + ta, bte, 11 - 2 * ta) for ta in range(4)]
                    else:
                        wt = [(2 * b - 2 + ta, bti, 11 - 2 * ta) for ta in range(6)]
                    tiles = wt + [(16, None, 0), (17, None, 0)]
                    halves = [tiles[0:4], tiles[4:]]
                    obank = 4 + cnt_o % 2
                    op_ = cnt_o % 2
                    cnt_o += 1
                    ssets = []
                    for hi, half in enumerate(halves):
                        sset = cnt_s % 2
                        cnt_s += 1
                        ssets.append(sset)
                        sb0 = 2 * sset
                        fns = []
                        for si, (T, tab, m0) in enumerate(half):
                            oap = psall[:, sb0 * 512 + si * 256: sb0 * 512 + (si + 1) * 256]
                            fns.append(I("matmul", oap, lhsT=kT[r0:r0 + 64, T * 128:(T + 1) * 128],
                                         rhs=qT[r0:r0 + 64, b * 256:(b + 1) * 256], start=True, stop=(tab is None)))
                            if tab is not None:
                                fns.append(I("matmul", oap, lhsT=cst["identb"][:], rhs=tab[:, hh, m0 * 64:(m0 + 4) * 64],
                                             start=False, stop=True))
                        P.mm(fns, reads=[kTb, qTb, btb, cstb], writes=[psb[sb0], psb[sb0 + 1]])
                        n = len(half)
                        P.op("act", I("activation", out=pt[sset][:, 0:n * 256], in_=psall[:, sb0 * 512: sb0 * 512 + n * 256],
                                      func=AF.Exp), reads=[psb[sb0], psb[sb0 + 1]], writes=[ptb[sset]])
                    for hi, half in enumerate(halves):
                        sset = ssets[hi]
                        n = len(half)
                        fns = []
                        for si, (T, tab, m0) in enumerate(half):
                            first = (hi == 0 and si == 0)
                            last = (hi == 1 and si == n - 1)
                            fns.append(I("matmul", PS(obank, 0, 256), lhsT=vaug[:, T, hh, :],
                                         rhs=pt[sset][:, si * 256:(si + 1) * 256], start=first, stop=last))
                        P.mm(fns, reads=[vb, ptb[sset]], writes=[psb[obank]])
                    P.op("dve", I("reciprocal", out=rc[op_][o0:o0 + 64, :],
                                                                             in_=PS(obank, 0, 256)[o0:o0 + 64, :]),
                         reads=[psb[obank]], writes=[rcb[op_]])
                    P.op("dve", I("tensor_tensor",
                        out=tt[op_][r0:r0 + 64, :], in0=PS(obank, 0, 256)[r0:r0 + 64, :], in1=rc[op_][o0:o0 + 64, :],
                        op=ALU.mult), reads=[psb[obank], rcb[op_]], writes=[rcb[op_]])
                    P.op("dve", I("tensor_tensor",
                        out=ynaT[par][r0:r0 + 64, b * 256:(b + 1) * 256], in0=tt[op_][r0:r0 + 64, :],
                        in1=szT[r0:r0 + 64, b * 256:(b + 1) * 256], op=ALU.mult),
                        reads=[rcb[op_], szb], writes=[ynb[par]])
            P.dma("sp", "yna_st%d" % par, ynaT_d[hp * 128:(hp + 1) * 128, :], ynaT[par][:], reads=[ynb[par]])

    if stage >= 3 and 'gates' not in skip:
        new_phase()
        wg = [A.alloc("wg%d" % i, [128, 8, 128], BF16) for i in range(2)]
        wgb = [Buf("wg%d" % i) for i in range(2)]
        sgt = [A.alloc("sgt%d" % i, [128, 512], F32) for i in range(2)]
        sgb = [Buf("sgt%d" % i) for i in range(2)]
        ci = 0
        for fc in range(16):
            par = fc % 2
            P.dma("pool", "wg%d" % par, wg[par][:], win_v[:, :, OGNA + fc * 128: OGNA + (fc + 1) * 128], writes=[wgb[par]])
            for tb in range(4):
                bank = 6 + ci % 2
                s_ = ci % 2
                ci += 1
                proj_fm(wg[par], wgb[par], bank, tb)
                P.op("act", I("activation", out=sgt[s_][:], in_=PS(bank), func=AF.Sigmoid),
                     reads=[psb[bank]], writes=[sgb[s_]])
                P.dma("sp", "sg_st%d" % s_, sg_d[fc * 128:(fc + 1) * 128, tb * 512:(tb + 1) * 512], sgt[s_][:],
                      reads=[sgb[s_]])

    if stage >= 4:
        new_phase()
        wdt = A.alloc("wdt", [128, 8, 64], BF16)
        dtb_bc = A.alloc("dtb_bc", [128, 64], F32)
        negA = A.alloc("negA", [128, 64], F32)
        dsk_bc = A.alloc("dsk_bc", [128, 32], F32)
        cwT = A.alloc("cwT", [128, 32, 5], F32)
        cbT = A.alloc("cbT", [128, 32], F32)
        dtv = A.alloc("dtv", [128, NTT, 64], F32)
        eF = A.alloc("eF", [128, NTT, 64], F32)
        wst = A.alloc("wst", [128, NTT, 64], F32)
        etot = A.alloc("etot", [128, NTT, 64], F32)
        av_hi = A.alloc("av_hi", [128, NTT, 64], BF16)
        av_lo = A.alloc("av_lo", [128, NTT, 64], BF16)
        mark1 = A.top
        av = A.alloc("av", [128, NTT, 64], F32)
        cum = A.alloc("cum", [128, NTT, 64], F32)
        tot = A.alloc("tot", [128, NTT, 64], F32)
        smb = Buf("ssd_small")
        dqb = Buf("dtq")
        P.dma("pool", "wdt", wdt[:], win_v[:, :, ODT:ODT + 64], writes=[smb])
        P.dma("sp", "ssd_small", dtb_bc[:], dtb_d.partition_broadcast(128), writes=[smb])
        P.dma("sp", "ssd_small", negA[:], alog_d.partition_broadcast(128), writes=[smb])
        P.dma("sp", "ssd_small", dsk_bc[:], dsk_d.partition_broadcast(128), writes=[smb])
        P.dma("sp", "ssd_small", cwT[:], cwT_d, writes=[smb])
        P.dma("sp", "ssd_small", cbT[:], cbT_d, writes=[smb])
        P.op("act", I("activation", out=negA[:], in_=negA[:], func=AF.Exp), reads=[smb], writes=[smb])
        P.op("dve", I("tensor_scalar", out=negA[:], in0=negA[:], scalar1=-1.0, scalar2=None, op0=ALU.mult),
             reads=[smb], writes=[smb])
        for t0 in range(0, NTT, 8):
            n = min(8, NTT - t0)
            bank = t0 // 8
            for i in range(n):
                T = t0 + i
                P.mm([I("matmul", PS(bank, i * 64, (i + 1) * 64), lhsT=htile(k, T),
                                                                   rhs=wdt[:, k, :], start=(k == 0), stop=(k == 7))
                      for k in range(8)], reads=[smb, hTb], writes=[psb[bank]])
            P.op("dve", I("tensor_tensor",
                out=dtv[:, t0:t0 + n, :], in0=PS(bank, 0, n * 64).rearrange("p (t c) -> p t c", c=64),
                in1=dtb_bc[:].unsqueeze(1).to_broadcast([128, n, 64]), op=ALU.add),
                reads=[psb[bank], smb], writes=[dqb])
        P.op("act", I("activation", out=cum[:], in_=dtv[:], func=AF.Exp), reads=[dqb], writes=[dqb])
        P.op("act", I("activation", out=dtv[:], in_=cum[:], func=AF.Ln, bias=1.0), reads=[dqb], writes=[dqb])
        P.op("dve", I("tensor_tensor", out=av[:], in0=dtv[:], in1=negA[:].unsqueeze(1).to_broadcast([128, NTT, 64]),
                                              op=ALU.mult), reads=[dqb, smb], writes=[dqb])
        for t0 in range(0, NTT, 4):
            n = min(4, NTT - t0)
            bank = t0 // 4
            fns = []
            for i in range(n):
                T = t0 + i
                c0 = i * 128
                fns.append(I("matmul", PS(bank, c0, c0 + 32), lhsT=cst["u_f"][:],
                                                                     rhs=av[:, T, 0:32], start=True, stop=True))
                fns.append(I("matmul", PS(bank, c0 + 32, c0 + 64), lhsT=cst["u_b"][:],
                                                                     rhs=av[:, T, 32:64], start=True, stop=True))
                fns.append(I("matmul", PS(bank, c0 + 64, c0 + 128), lhsT=cst["ones"][:],
                                                                     rhs=av[:, T, :], start=True, stop=True))
            P.mm(fns, reads=[dqb, cstb], writes=[psb[bank]])
            src = PS(bank, 0, n * 128).rearrange("p (t c) -> p t c", c=128)
            P.op("dve", I("tensor_copy", out=cum[:, t0:t0 + n, :], in_=src[:, :, 0:64]),
                 reads=[psb[bank]], writes=[dqb])
            P.op("act", I("activation", out=tot[:, t0:t0 + n, :], in_=src[:, :, 64:128],
                                                                    func=AF.Copy), reads=[psb[bank]], writes=[dqb])
        P.op("act", I("activation", out=eF[:], in_=cum[:], func=AF.Exp), reads=[dqb], writes=[dqb])
        P.op("act", I("activation", out=etot[:], in_=tot[:], func=AF.Exp), reads=[dqb], writes=[dqb])
        P.op("dve", I("tensor_tensor", out=cum[:], in0=tot[:], in1=cum[:], op=ALU.subtract), reads=[dqb], writes=[dqb])
        P.op("act", I("activation", out=cum[:], in_=cum[:], func=AF.Exp), reads=[dqb], writes=[dqb])
        P.op("dve", I("tensor_tensor", out=wst[:], in0=dtv[:], in1=cum[:], op=ALU.mult), reads=[dqb], writes=[dqb])
        P.op("dve", I("tensor_copy", out=av_hi[:], in_=av[:]), reads=[dqb], writes=[dqb])
        P.op("dve", I("tensor_tensor", out=av_lo[:], in0=av[:], in1=av_hi[:], op=ALU.subtract), reads=[dqb], writes=[dqb])
        dbg_out("dtv", dtv[:], [128, NTT, 64], F32, [dqb])
        dbg_out("eF", eF[:], [128, NTT, 64], F32, [dqb])
        dbg_out("wst", wst[:], [128, NTT, 64], F32, [dqb])
        dbg_out("etot", etot[:], [128, NTT, 64], F32, [dqb])
        P.barrier()
        A.top = mark1
        wx = [A.alloc("wx0", [128, 8, 256], BF16)] * 2
        wB = [A.alloc("wB0", [128, 8, 128], BF16)] * 2
        wC = [A.alloc("wC0", [128, 8, 128], BF16)] * 2
        wzs = [A.alloc("wzs%d" % i, [128, 8, 256], BF16) for i in range(2)]
        cbr = [A.alloc("cbr%d" % i, [1, 384], BF16) for i in range(2)]
        sng_g = [A.alloc("sng_g%d" % i, [128, 256], F32) for i in range(2)]
        wgb_ = [[Buf("wg%d_%d" % (j, i)) for i in range(2)] for j in range(6)]
        for j in range(3):
            wgb_[j][1] = wgb_[j][0]
        dgc = A.alloc("dgc", [128, 4, 5, 128], BF16)
        dgcb = Buf("dgc")
        XW = 2 + S + 2 + 2 + CL + 2
        xbcT = [A.alloc("xbcT0", [128, XW], BF16)] * 2
        xbb = [Buf("xbcT0")] * 2
        xsB = A.alloc("xsB", [128, NTT, 384], BF16)
        xsBb = Buf("xsB")
        BCT = A.alloc("BCT", [128, 2, S], BF16)
        BCTb = Buf("BCT")
        Hs = A.alloc("Hs", [128, 2, NT, 256], BF16)
        Hsb = Buf("Hs")
        Hf = A.alloc("Hf", [128, 2, 256], F32)
        Hfb = [Buf("Hf0"), Buf("Hf1")]
        xw = [[A.alloc("xw%d_%d" % (d, i), [128, 256], BF16) for i in range(2)] for d in range(2)]
        xwb = [[Buf("xw%d_%d" % (d, i)) for i in range(2)] for d in range(2)]
        aU = [[A.alloc("aU%d_%d" % (d, i), [128, 8, 128], BF16) for i in range(2)] for d in range(2)]
        aUb = [[Buf("aU%d_%d" % (d, i)) for i in range(2)] for d in range(2)]
        aUb2 = [[Buf("aUlo%d_%d" % (d, i)) for i in range(2)] for d in range(2)]
        Et = [A.alloc("Et%d" % i, [128, 512], F32) for i in range(2)]
        Etb = [Buf("Et%d" % i) for i in range(2)]
        Mt = [[A.alloc("Mt%d_%d" % (d, i), [128, 4, 128], BF16) for i in range(2)] for d in range(2)]
        Mtb = [[Buf("Mt%d_%d" % (d, i)) for i in range(2)] for d in range(2)]
        xdt = [[A.alloc("xdt%d_%d" % (d, i), [128, 256], BF16) for i in range(2)] for d in range(2)]
        xdtb = [[Buf("xdt%d_%d" % (d, i)) for i in range(2)] for d in range(2)]
        xsk = [A.alloc("xsk%d" % i, [128, 256], BF16) for i in range(2)]
        xskb = [Buf("xsk%d" % i) for i in range(2)]
        szs = [A.alloc("szs%d" % i, [128, 256], F32) for i in range(2)]
        szsb = [Buf("szs%d" % i) for i in range(2)]
        t1 = [A.alloc("t1_%d" % i, [128, 256], F32) for i in range(2)]
        t2 = [A.alloc("t2_%d" % i, [128, 256], F32) for i in range(2)]
        yv = [A.alloc("yv%d" % i, [128, 256], F32) for i in range(2)]
        th, zh = t1, t2
        yn = [A.alloc("yn%d" % i, [128, 256], BF16) for i in range(2)]
        cmb = [Buf("combine%d" % i) for i in range(2)]
        ynb_ = [Buf("yn%d" % i) for i in range(2)]
        st = [A.alloc("st%d" % i, [128, 4], F32) for i in range(2)]
        stb = [Buf("st%d" % i) for i in range(2)]
        nhalf = A.alloc("nhalf", [128, 1], F32)
        P.op("pool", I("memset", nhalf[:], -0.5), writes=[smb])
        yssT = [A.alloc("yssT%d" % i, [128, 2, 128], BF16) for i in range(2)]
        yssb = [Buf("yssT%d" % i) for i in range(2)]
        P.op("pool", I("memset", xbcT[0][:], 0.0), writes=[xbb[0]])
        LAT0 = 2
        CTX0 = 2 + S + 2 + 2

        def tokbase(T):
            return LAT0 + T * 128 if T < NT else CTX0 + (T - NT) * 128

        xi = 0
        for g in range(groups):
            par = g % 2
            cids = [2 * g, 2 * g + 1, 16 + g, 24 + g]
            P.dma("pool", "wg0_%d" % par, wx[par][:], win_v[:, :, OXS + g * 256: OXS + (g + 1) * 256], writes=[wgb_[0][par]])
            P.dma("pool", "wg1_%d" % par, wB[par][:], win_v[:, :, OB + g * 128: OB + (g + 1) * 128], writes=[wgb_[1][par]])
            P.dma("pool", "wg2_%d" % par, wC[par][:], win_v[:, :, OC + g * 128: OC + (g + 1) * 128], writes=[wgb_[2][par]])
            P.dma("pool", "wg3_%d" % par, wzs[par][:], win_v[:, :, OZS + g * 256: OZS + (g + 1) * 256], writes=[wgb_[3][par]])
            P.dma("pool", "wg4_%d" % par, cbr[par][0:1, 0:256], cbrow_d[0:1, g * 256:(g + 1) * 256], writes=[wgb_[4][par]])
            P.dma("pool", "wg4_%d" % par, cbr[par][0:1, 256:384], cbrow_d[0:1, 2048 + g * 128: 2048 + (g + 1) * 128],
                  writes=[wgb_[4][par]])
            P.dma("sp", "wg5_%d" % par, sng_g[par][:], sng_d[0:1, g * 256:(g + 1) * 256].partition_broadcast(128),
                  writes=[wgb_[5][par]])
            for ci in range(4):
                for k in range(5):
                    P.op("pool", I("tensor_scalar",
                        out=dgc[:, ci, k, :], in0=cst["identb"][:], scalar1=cwT[:, cids[ci], k:k + 1], scalar2=0.0,
                        op0=ALU.mult, op1=ALU.add), reads=[cstb, smb], writes=[dgcb])
            for ci in range(4):
                w_ap = (wx[par][:, :, 0:128], wx[par][:, :, 128:256], wB[par], wC[par])[ci]
                wbuf = (wgb_[0][par], wgb_[0][par], wgb_[1][par], wgb_[2][par])[ci]
                xs_ = xi % 2
                xi += 1
                xb_ = xbcT[xs_]
                for tb in range(4):
                    bank = 6 + tb % 2
                    proj_fm(w_ap, wbuf, bank, tb)
                    evac_copy(xb_[:, LAT0 + tb * 512: LAT0 + (tb + 1) * 512], PS(bank), [psb[bank]], [xbb[xs_]])
                if ci < 3:
                    proj_fm(w_ap, wbuf, 6, 0, ctx=True)
                    evac_copy(xb_[:, CTX0: CTX0 + CL], PS(6, 0, CL), [psb[6]], [xbb[xs_]])
                if ci < 3:
                    for T in range(NTT):
                        bank = 4 + T % 2
                        base = tokbase(T)
                        fns = [I("matmul",
                            PS(bank, 0, 128), lhsT=xb_[:, base + k - 2: base + k - 2 + 128], rhs=dgc[:, ci, k, :],
                            start=(k == 0), stop=False) for k in range(5)]
                        fns.append(I("matmul", PS(bank, 0, 128), lhsT=cst["onesb"][0:1, :],
                                                                 rhs=cbr[par][0:1, ci * 128:(ci + 1) * 128],
                                                                 start=False, stop=True))
                        P.mm(fns, reads=[xbb[xs_], dgcb, cstb, wgb_[4][par]], writes=[psb[bank]])
                        P.op("act", I("activation", out=xsB[:, T, ci * 128:(ci + 1) * 128],
                                                                           in_=PS(bank, 0, 128), func=AF.Silu),
                             reads=[psb[bank]], writes=[xsBb])
                if ci >= 2:
                    for tb in range(4):
                        bank = 4 + tb % 2
                        b0_ = LAT0 + tb * 512
                        P.mm([I("matmul",
                            PS(bank), lhsT=dgc[:, ci, k, :], rhs=xb_[:, b0_ + k - 2: b0_ + k - 2 + 512],
                            start=(k == 0), stop=(k == 4)) for k in range(5)],
                            reads=[xbb[xs_], dgcb], writes=[psb[bank]])
                        P.op("act", I("activation",
                            out=BCT[:, ci - 2, tb * 512:(tb + 1) * 512], in_=PS(bank), func=AF.Silu,
                            bias=cbT[:, cids[ci]:cids[ci] + 1]), reads=[psb[bank], smb], writes=[BCTb])
            if g == 0:
                dbg_out("xsB", xsB[:], [128, NTT, 384], BF16, [xsBb])
                dbg_out("BCT", BCT[:], [128, 2, S], BF16, [BCTb])
            orders = [[16, 17] + list(range(16)), [17, 16] + list(range(15, -1, -1))]
            for d in range(2):
                P.op("pool", I("memset", Hf[:, d, :], 0.0), writes=[Hfb[d]])

            def st_pre(d, idx):
                T = orders[d][idx]
                sl = idx % 2
                c0 = d * 32 + 4 * g
                bank = 2 + 2 * d + sl
                P.op("pool", I("tensor_tensor", out=xw[d][sl][:].rearrange("p (r c) -> p r c", c=64),
                              in0=xsB[:, T, 0:256].rearrange("p (r c) -> p r c", c=64),
                              in1=wst[:, T, c0:c0 + 4].unsqueeze(2).to_broadcast([128, 4, 64]), op=ALU.mult),
                     reads=[xsBb, dqb], writes=[xwb[d][sl]])
                P.mm([I("matmul", PS(bank, 0, 256), lhsT=xsB[:, T, 256:384], rhs=xw[d][sl][:], start=True, stop=True)],
                     reads=[xsBb, xwb[d][sl]], writes=[psb[bank]])

            def st_post(d, idx):
                T = orders[d][idx]
                sl = idx % 2
                c0 = d * 32 + 4 * g
                bank = 2 + 2 * d + sl
                if T < NT:
                    P.op("act", I("activation", out=Hs[:, d, T, :], in_=Hf[:, d, :], func=AF.Copy),
                         reads=[Hfb[d]], writes=[Hsb])
                if idx == NTT - 1:
                    return
                P.op("dve", I("tensor_tensor", out=Hf[:, d, :].rearrange("p (r c) -> p r c", c=64),
                              in0=Hf[:, d, :].rearrange("p (r c) -> p r c", c=64),
                              in1=etot[:, T, c0:c0 + 4].unsqueeze(2).to_broadcast([128, 4, 64]), op=ALU.mult),
                     reads=[dqb], writes=[Hfb[d]])
                P.op("dve", I("tensor_tensor", out=Hf[:, d, :], in0=Hf[:, d, :], in1=PS(bank, 0, 256), op=ALU.add),
                     reads=[psb[bank]], writes=[Hfb[d]])

            for d in range(2):
                st_pre(d, 0)
            for idx in range(NTT):
                if idx + 1 < NTT - 1:
                    for d in range(2):
                        st_pre(d, idx + 1)
                for d in range(2):
                    st_post(d, idx)
            if g == 0:
                dbg_out("Hs", Hs[:], [128, 2, NT, 256], BF16, [Hsb])
            def out_aU(c):
                p_ = c % 2
                for d in range(2):
                    c0 = d * 32 + 4 * g
                    uu = cst["ub_f"] if d == 0 else cst["ub_b"]
                    P.op("pool", I("tensor_tensor", out=aU[d][p_][:, 0:4, :], in0=uu[:].unsqueeze(1).to_broadcast([128, 4, 128]),
                                   in1=av_hi[:, c, c0:c0 + 4].unsqueeze(2).to_broadcast([128, 4, 128]), op=ALU.mult),
                         reads=[cstb, dqb], writes=[aUb[d][p_]])
                    P.op("dve" if d == 0 else "pool",
                         I("tensor_tensor", out=aU[d][p_][:, 4:8, :], in0=uu[:].unsqueeze(1).to_broadcast([128, 4, 128]),
                           in1=av_lo[:, c, c0:c0 + 4].unsqueeze(2).to_broadcast([128, 4, 128]), op=ALU.mult),
                         reads=[cstb, dqb], writes=[aUb2[d][p_]])

            def out_front(c):
                p_ = c % 2
                bA, bB, bC = 2 + p_, 4 + p_, 6 + p_
                tsl = slice(c * 128, (c + 1) * 128)
                P.mm([I("matmul", PS(bA, 0, 128), lhsT=BCT[:, 0, tsl], rhs=BCT[:, 1, tsl], start=True, stop=True)],
                     reads=[BCTb], writes=[psb[bA]])
                P.op("pool", I("tensor_tensor", out=xsk[p_][:].rearrange("p (r c) -> p r c", c=64),
                               in0=xsB[:, c, 0:256].rearrange("p (r c) -> p r c", c=64),
                               in1=dsk_bc[:, 4 * g:4 * g + 4].unsqueeze(2).to_broadcast([128, 4, 64]), op=ALU.mult),
                     reads=[xsBb, smb], writes=[xskb[p_]])
                P.mm([I("matmul", PS(bB, 0, 256), lhsT=cst["identb"][:], rhs=xsk[p_][:], start=True, stop=False)],
                     reads=[cstb, xskb[p_]], writes=[psb[bB]])
                P.mm([I("matmul", PS(bA, 128, 384), lhsT=htile(k, c), rhs=wzs[par][:, k, :],
                        start=(k == 0), stop=(k == 7)) for k in range(8)],
                     reads=[wgb_[3][par], hTb], writes=[psb[bA]])
                P.op("act", I("activation", out=th[p_][:], in_=PS(bA, 128, 384), func=AF.Tanh, scale=0.5),
                     reads=[psb[bA]], writes=[cmb[p_]])
                P.op("act", I("activation", out=zh[p_][:], in_=PS(bA, 128, 384), func=AF.Copy, scale=0.5),
                     reads=[psb[bA]], writes=[cmb[p_]])
                P.op("dve", I("scalar_tensor_tensor", out=szs[p_][:], in0=th[p_][:], scalar=1.0, in1=zh[p_][:],
                              op0=ALU.add, op1=ALU.mult), reads=[cmb[p_]], writes=[szsb[p_]])
                for d in range(2):
                    c0 = d * 32 + 4 * g
                    uu = cst["ub_f"] if d == 0 else cst["ub_b"]
                    ls = cst["lsb_f"] if d == 0 else cst["lsb_b"]
                    mi = cst["mi_f"] if d == 0 else cst["mi_b"]
                    P.mm([I("matmul", PS(d), lhsT=ls[:], rhs=aU[d][p_][:, 0:4, :].rearrange("p r c -> p (r c)"), start=True, stop=False),
                          I("matmul", PS(d), lhsT=ls[:], rhs=aU[d][p_][:, 4:8, :].rearrange("p r c -> p (r c)"), start=False, stop=False),
                          I("matmul", PS(d), lhsT=cst["negib"][:], rhs=mi[:], start=False, stop=True)],
                         reads=[aUb[d][p_], aUb2[d][p_], cstb], writes=[psb[d]])
                    P.op("act", I("activation", out=Et[d][:], in_=PS(d), func=AF.Exp), reads=[psb[d]], writes=[Etb[d]])
                for d in range(2):
                    c0 = d * 32 + 4 * g
                    P.op("dve", I("tensor_tensor", out=Mt[d][p_][:], in0=Et[d][:].rearrange("p (r c) -> p r c", c=128),
                                  in1=PS(bA, 0, 128).unsqueeze(1).to_broadcast([128, 4, 128]), op=ALU.mult),
                         reads=[Etb[d], psb[bA]], writes=[Mtb[d][p_]])
                    P.op("pool", I("tensor_tensor", out=xdt[d][p_][:].rearrange("p (r c) -> p r c", c=64),
                                   in0=xsB[:, c, 0:256].rearrange("p (r c) -> p r c", c=64),
                                   in1=dtv[:, c, c0:c0 + 4].unsqueeze(2).to_broadcast([128, 4, 64]), op=ALU.mult),
                         reads=[xsBb, dqb], writes=[xdtb[d][p_]])

            def out_front2(c):
                p_ = c % 2
                bA, bB, bC = 2 + p_, 4 + p_, 6 + p_
                tsl = slice(c * 128, (c + 1) * 128)
                for d in range(2):
                    P.mm([I("matmul", PS(bB, r * 64, (r + 1) * 64), lhsT=Mt[d][p_][:, r, :],
                            rhs=xdt[d][p_][:, r * 64:(r + 1) * 64], start=False, stop=(d == 1 and r == 3)) for r in range(4)],
                         reads=[Mtb[d][p_], xdtb[d][p_]], writes=[psb[bB]])
                    P.mm([I("matmul", PS(bC, d * 256, (d + 1) * 256), lhsT=BCT[:, 1, tsl], rhs=Hs[:, d, c, :],
                            start=True, stop=True)], reads=[BCTb, Hsb], writes=[psb[bC]])

            def out_back(c):
                p_ = c % 2
                bA, bB, bC = 2 + p_, 4 + p_, 6 + p_
                tsl = slice(c * 128, (c + 1) * 128)
                f0 = 4 * g
                b0c = 32 + 4 * g
                P.op("dve", I("tensor_tensor", out=t1[p_][:].rearrange("p (r c) -> p r c", c=64),
                              in0=PS(bC, 0, 256).rearrange("p (r c) -> p r c", c=64),
                              in1=eF[:, c, f0:f0 + 4].unsqueeze(2).to_broadcast([128, 4, 64]), op=ALU.mult),
                     reads=[psb[bC], dqb], writes=[cmb[p_]])
                P.op("dve", I("tensor_tensor", out=t2[p_][:].rearrange("p (r c) -> p r c", c=64),
                              in0=PS(bC, 256, 512).rearrange("p (r c) -> p r c", c=64),
                              in1=eF[:, c, b0c:b0c + 4].unsqueeze(2).to_broadcast([128, 4, 64]), op=ALU.mult),
                     reads=[psb[bC], dqb, cmb[p_]], writes=[cmb[p_]])
                P.op("dve", I("tensor_tensor", out=t1[p_][:], in0=t1[p_][:], in1=t2[p_][:], op=ALU.add),
                     reads=[cmb[p_]], writes=[cmb[p_]])
                P.op("dve", I("tensor_tensor", out=yv[p_][:], in0=t1[p_][:], in1=PS(bB, 0, 256), op=ALU.add),
                     reads=[cmb[p_], psb[bB]], writes=[cmb[p_]])
                if g == 0 and c == 5:
                    dbg_out("yv", yv[p_][:], [128, 256], F32, [cmb[p_]])
                P.op("dve", I("tensor_tensor", out=yv[p_][:], in0=yv[p_][:], in1=szs[p_][:], op=ALU.mult),
                     reads=[cmb[p_], szsb[p_]], writes=[cmb[p_]])
                P.op("act", I("activation", out=t2[p_][:], in_=yv[p_][:], func=AF.Square, accum_out=st[p_][:, 0:1]),
                     reads=[cmb[p_]], writes=[cmb[p_], stb[p_]])

            def out_back_b(c):
                p_ = c % 2
                bA, bB, bC = 2 + p_, 4 + p_, 6 + p_
                tsl = slice(c * 128, (c + 1) * 128)
                P.op("pool", I("tensor_scalar", out=st[p_][:, 1:2], in0=st[p_][:, 0:1], scalar1=1.0 / 256, scalar2=EPS,
                               op0=ALU.mult, op1=ALU.add), reads=[stb[p_]], writes=[stb[p_]])
                P.op("pool", I("tensor_tensor", out=st[p_][:, 2:3], in0=st[p_][:, 1:2], in1=nhalf[:], op=ALU.pow),
                     reads=[stb[p_], smb], writes=[stb[p_]])

            def out_back_c1(c):
                p_ = c % 2
                P.op("dve", I("scalar_tensor_tensor", out=yn[p_][:], in0=yv[p_][:], scalar=st[p_][:, 2:3],
                              in1=sng_g[par][:], op0=ALU.mult, op1=ALU.mult),
                     reads=[cmb[p_], stb[p_], wgb_[5][par]], writes=[ynb_[p_]])

            def out_back_c(c):
                p_ = c % 2
                bA, bB, bC = 2 + p_, 4 + p_, 6 + p_
                tsl = slice(c * 128, (c + 1) * 128)
                P.mm([I("matmul", PS(bC, j * 128, (j + 1) * 128), lhsT=yn[p_][:, j * 128:(j + 1) * 128],
                        rhs=cst["identb"][:], start=True, stop=True) for j in range(2)],
                     reads=[ynb_[p_], cstb], writes=[psb[bC]])
                P.op("act", I("activation", out=yssT[p_][:], in_=PS(bC, 0, 256).rearrange("p (j c) -> p j c", c=128),
                              func=AF.Copy), reads=[psb[bC]], writes=[yssb[p_]])
                for j in range(2):
                    P.dma("sp", "yss_st%d" % p_, yssT_d[(2 * g + j) * 128:(2 * g + j + 1) * 128, tsl], yssT[p_][:, j, :],
                          reads=[yssb[p_]])

            out_aU(0)
            out_aU(1)
            out_front(0)
            for c in range(NT):
                if c >= 1:
                    out_back_b(c - 1)
                    out_back_c1(c - 1)
                if c + 2 < NT:
                    out_aU(c + 2)
                if c + 1 < NT:
                    out_front(c + 1)
                if c >= 1:
                    out_back_c(c - 1)
                out_front2(c)
                out_back(c)
            out_back_b(NT - 1)
            out_back_c1(NT - 1)
            out_back_c(NT - 1)


    if stage >= 5:
        new_phase()
        A.top = mark_h
        wna = A.alloc("wna", [128, 8, D], BF16)
        wss = A.alloc("wss", [128, 16, D], BF16)
        wo = A.alloc("wo", [128, 8, D], BF16)
        web = Buf("we")
        P.dma("pool", "we0", wna[:], wna_d.rearrange("(c p) n -> p c n", p=128), writes=[web])
        for hf in range(2):
            P.dma("pool", "we1", wss[:, hf * 8:(hf + 1) * 8, :],
                  wss_d[hf * 1024:(hf + 1) * 1024, :].rearrange("(c p) n -> p c n", p=128), writes=[web])
        P.dma("pool", "we2", wo[:], wout_d.rearrange("(c p) n -> p c n", p=128), writes=[web])
        ybk2 = [A.alloc("ybk%d" % i, [128, 24, 512], BF16) for i in range(2)]
        ybb2 = [Buf("ybk%d" % i) for i in range(2)]
        sgk = [A.alloc("sgk%d" % i, [128, 2, 512], F32) for i in range(2)]
        sgkb = [Buf("sgk%d" % i) for i in range(2)]
        mTt = A.alloc("mTt", [128, 8, 512], BF16)
        mTb = Buf("mTt")
        e1 = [A.alloc("e1_%d" % i, [128, 512], F32) for i in range(2)]
        e2 = [A.alloc("e2_%d" % i, [128, 512], F32) for i in range(2)]
        eb = [Buf("e%d" % i) for i in range(2)]
        xr = [A.alloc("xr%d" % i, [128, D], F32) for i in range(2)]
        xrb = [Buf("xr%d" % i) for i in range(2)]
        ot = [A.alloc("ot%d" % i, [128, D], F32) for i in range(2)]
        otb = [Buf("ot%d" % i) for i in range(2)]
        junk2 = A.alloc("junk2", [128, D], BF16)
        j2b = Buf("junk2")
        s2 = [A.alloc("s2_%d" % i, [128, 4], F32) for i in range(2)]
        s2b = [Buf("s2_%d" % i) for i in range(2)]
        ei = 0
        for tb in range(4):
            tcs = slice(tb * 512, (tb + 1) * 512)
            ybk, ybb = ybk2[tb % 2], ybb2[tb % 2]
            for tb_l in ([0, 1] if tb == 0 else ([tb + 1] if tb + 1 < 4 else [])):
                tcl = slice(tb_l * 512, (tb_l + 1) * 512)
                for cc in range(8):
                    P.dma("sp", "ybk%d" % (tb_l % 2), ybk2[tb_l % 2][:, cc, :], ynaT_d[cc * 128:(cc + 1) * 128, tcl],
                          writes=[ybb2[tb_l % 2]])
                for cc in range(16):
                    P.dma("sp", "ybk%d" % (tb_l % 2), ybk2[tb_l % 2][:, 8 + cc, :], yssT_d[cc * 128:(cc + 1) * 128, tcl],
                          writes=[ybb2[tb_l % 2]])
            for f in range(8):
                s_ = ei % 2
                ei += 1
                fsl = slice(f * 128, (f + 1) * 128)
                P.dma("act", "sgk%d" % s_, sgk[s_][:, 0, :], sg_d[f * 128:(f + 1) * 128, tcs], writes=[sgkb[s_]])
                P.dma("act", "sgk%d" % s_, sgk[s_][:, 1, :], sg_d[1024 + f * 128: 1024 + (f + 1) * 128, tcs], writes=[sgkb[s_]])
                b1 = 2 * s_
                P.mm([I("matmul", PS(b1), lhsT=wna[:, cc, fsl], rhs=ybk[:, cc, :],
                                                               start=(cc == 0), stop=(cc == 7)) for cc in range(8)],
                     reads=[web, ybb], writes=[psb[b1]])
                P.mm([I("matmul", PS(b1 + 1), lhsT=wss[:, cc, fsl], rhs=ybk[:, 8 + cc, :],
                                                               start=(cc == 0), stop=(cc == 15)) for cc in range(16)],
                     reads=[web, ybb], writes=[psb[b1 + 1]])
                P.op("dve", I("tensor_tensor", out=e1[s_][:], in0=PS(b1), in1=sgk[s_][:, 0, :], op=ALU.mult),
                     reads=[psb[b1], sgkb[s_]], writes=[eb[s_]])
                P.op("dve", I("tensor_tensor", out=e2[s_][:], in0=PS(b1 + 1), in1=sgk[s_][:, 1, :], op=ALU.mult),
                     reads=[psb[b1 + 1], sgkb[s_]], writes=[eb[s_]])
                P.op("pool", I("tensor_tensor", out=mTt[:, f, :], in0=e1[s_][:], in1=e2[s_][:], op=ALU.add),
                     reads=[eb[s_]], writes=[mTb])
            if tb == 0:
                dbg_out("mTt", mTt[:], [128, 8, 512], BF16, [mTb])
            for tl in range(4):
                T = tb * 4 + tl
                s_ = T % 2
                P.dma("act", "xr%d" % s_, xr[s_][:], x_d[T * 128:(T + 1) * 128, :], writes=[xrb[s_]])
                for hf in range(2):
                    bank = 4 + 2 * s_ + hf
                    P.mm([I("matmul",
                        PS(bank), lhsT=mTt[:, f, tl * 128:(tl + 1) * 128], rhs=wo[:, f, hf * 512:(hf + 1) * 512],
                        start=(f == 0), stop=(f == 7)) for f in range(8)],
                        reads=[mTb, web], writes=[psb[bank]])
                mer = psall[:, (4 + 2 * s_) * 512:(4 + 2 * s_) * 512 + 1024]
                P.op("act", I("activation", out=junk2[:], in_=mer, func=AF.Square,
                                                                    accum_out=s2[s_][:, 0:1]),
                     reads=[psb[4 + 2 * s_], psb[5 + 2 * s_]], writes=[j2b, s2b[s_]])
                P.op("act", I("activation", out=s2[s_][:, 1:2], in_=s2[s_][:, 0:1], func=AF.Sqrt,
                                                          scale=1.0 / D, bias=EPS), reads=[s2b[s_]], writes=[s2b[s_]])
                P.op("dve", I("reciprocal", out=s2[s_][:, 2:3], in_=s2[s_][:, 1:2]),
                     reads=[s2b[s_]], writes=[s2b[s_]])
                P.op("dve", I("scalar_tensor_tensor", out=ot[s_][:], in0=mer, scalar=s2[s_][:, 2:3],
                                                                              in1=ggp[:], op0=ALU.mult, op1=ALU.mult),
                     reads=[psb[4 + 2 * s_], psb[5 + 2 * s_], s2b[s_], modb], writes=[otb[s_]])
                P.op("pool", I("tensor_tensor", out=ot[s_][:], in0=ot[s_][:], in1=xr[s_][:], op=ALU.add),
                     reads=[xrb[s_]], writes=[otb[s_]])
                P.dma("sp", "out_st%d" % s_, out_d[T * 128:(T + 1) * 128, :], ot[s_][:], reads=[otb[s_]])

    P.barrier(["sp"])
    P.emit()
    print("sbuf peak", A.peak, "instr", {k: len(v) for k, v in P.q.items()})
    return nc, list(dbg_d.keys())


def _prep_inputs(inputs):
    f = lambda a: np.ascontiguousarray(np.asarray(a, dtype=np.float32))
    x = f(inputs["x"]); c = f(inputs["c"]); ctx = f(inputs["ctx"]); c_ctx = f(inputs["c_ctx"])
    shared = {
        "w_mod": f(inputs["w_mod"][0]),
        "b_mod": f(inputs["b_mod"][0]).reshape(1, -1),
        "g_preT": f(f(inputs["g_pre"][0]).reshape(8, 128).T),
        "g_post": f(inputs["g_post"][0]).reshape(1, -1),
        "w_in": f(inputs["w_in"][0]),
        "conv_wT": f(f(inputs["conv_w"][0]).reshape(5, 32, 128).transpose(2, 1, 0)),
        "conv_bT": f(f(inputs["conv_b"][0]).reshape(32, 128).T),
        "conv_brow": f(inputs["conv_b"][0]).reshape(1, -1),
        "a_log": f(inputs["a_log"][0]).reshape(1, 64),
        "dt_bias": f(inputs["dt_bias"][0]).reshape(1, 64),
        "d_skip": f(inputs["d_skip"][0]).reshape(1, 32),
        "ssm_norm_g": f(inputs["ssm_norm_g"][0]).reshape(1, -1),
        "rpbT": _rpb_layout(f(inputs["rpb"][0])),
        "w_na_out": f(inputs["w_na_out"][0]),
        "w_ssm_out": f(inputs["w_ssm_out"][0]),
        "w_out": f(inputs["w_out"][0]),
    }
    for k, v in _consts().items():
        shared["c_" + k] = v
    maps = []
    for b in range(x.shape[0]):
        cv = np.stack([c[b], c_ctx], axis=-1)
        m = dict(shared)
        m["x"] = x[b]
        m["ctx"] = ctx[b]
        m["cvT"] = f(cv.reshape(8, 128, 2).transpose(1, 0, 2))
        maps.append(m)
    return maps


def kernel(**inputs):
    maps = _prep_inputs(inputs)
    nc, _ = build()
    res = run_bass_kernel_spmd(nc, maps, core_ids=list(range(8)))
    return np.stack([np.asarray(r["out"], dtype=np.float32) for r in res.results], axis=0)
```

```python
import numpy as np
import ml_dtypes
import concourse.bass as bass
import concourse.mybir as mybir
from concourse.bass_utils import run_bass_kernel_spmd
from concourse.alu_op_type import AluOpType as ALU

F32 = mybir.dt.float32
BF16 = mybir.dt.bfloat16
AF = mybir.ActivationFunctionType

D = 1024
S = 2048
CL = 256
NT = 16
NTT = 18
PW = 12352
NEG = -30000.0
EPS = 1e-6
OQ, OK_, OV, OZNA, OZS, OXS, OB, OC, ODT, OGNA, OGS = 0, 1024, 2048, 3072, 4096, 6144, 8192, 9216, 10240, 10304, 11328


def I(m, *a, **kw):
    return (m, a, kw)


class Buf:
    __slots__ = ("name", "writers", "readers")

    def __init__(self, name):
        self.name = name
        self.writers = []
        self.readers = []


class Prog:
    ENGS = ("pe", "act", "dve", "pool", "sp")

    def __init__(self, nc):
        self.nc = nc
        self.q = {e: [] for e in self.ENGS}
        self.sem = {e: nc.alloc_semaphore("sem_" + e) for e in self.ENGS}
        self.cnt = {e: 0 for e in self.ENGS}
        self.waited = {e: {} for e in self.ENGS}
        self.dsem = {}
        self.semh = dict(self.sem)
        self.last = {}

    def _need(self, eng, toks):
        best = {}
        for (k, v) in toks:
            if k == eng and eng == "pe":
                continue
            if best.get(k, 0) < v:
                best[k] = v
        for k, v in best.items():
            if self.waited[eng].get(k, 0) >= v:
                continue
            self.waited[eng][k] = v
            h = self.semh[k]
            self.q[eng].append(lambda e, h=h, v=v: e.wait_ge(h, v))

    def op(self, eng, fn, reads=(), writes=()):
        toks = []
        for b in reads:
            toks += b.writers
        for b in writes:
            toks += [t for t in b.writers if t[0] != eng]
            toks += [t for t in b.readers if t[0] != eng]
        self._need(eng, toks)
        self.cnt[eng] += 1
        tok = (eng, self.cnt[eng])
        self.last[eng] = tok
        h = self.sem[eng]
        self.q[eng].append(lambda e, fn=fn, h=h: getattr(e, fn[0])(*fn[1], **fn[2]).then_inc(h, 1))
        for b in reads:
            b.readers.append(tok)
        for b in writes:
            b.writers = [tok]
            b.readers = []
        return tok

    def mm(self, fns, reads=(), writes=()):
        toks = []
        for b in reads:
            toks += b.writers
        for b in writes:
            toks += b.writers
            toks += b.readers
        self._need("pe", toks)
        self.cnt["pe"] += 1
        tok = ("pe", self.cnt["pe"])
        self.last["pe"] = tok
        h = self.sem["pe"]
        n = len(fns)
        for i, fn in enumerate(fns):
            if i == n - 1:
                self.q["pe"].append(lambda e, fn=fn, h=h: getattr(e, fn[0])(*fn[1], **fn[2]).then_inc(h, 1))
            else:
                self.q["pe"].append(lambda e, fn=fn: getattr(e, fn[0])(*fn[1], **fn[2]))
        for b in reads:
            b.readers.append(tok)
        for b in writes:
            b.writers = [tok]
            b.readers = []
        return tok

    def dma(self, eng, semname, out, in_, reads=(), writes=(), **kw):
        if semname not in self.dsem:
            h = self.nc.alloc_semaphore("dsem_" + semname)
            self.dsem[semname] = [h, 0]
            self.semh["d:" + semname] = h
        ent = self.dsem[semname]
        toks = []
        for b in reads:
            toks += b.writers
        for b in writes:
            toks += b.writers
            toks += b.readers
        self._need(eng, toks)
        ent[1] += 16
        tok = ("d:" + semname, ent[1])
        self.last["d:" + semname] = tok
        h = ent[0]
        self.q[eng].append(
            lambda e, out=out, in_=in_, h=h, kw=kw: e.dma_start(out=out, in_=in_, **kw).then_inc(h, 16))
        for b in reads:
            b.readers.append(tok)
        for b in writes:
            b.writers = [tok]
            b.readers = []
        return tok

    def barrier(self, engs=None):
        toks = list(self.last.values())
        for e in (engs or self.ENGS):
            self._need(e, toks)

    def emit(self):
        nc = self.nc
        with nc.Block() as block:
            @block.sync
            def _(e):
                for f in self.q["sp"]:
                    f(e)

            @block.tensor
            def _(e):
                for f in self.q["pe"]:
                    f(e)

            @block.scalar
            def _(e):
                for f in self.q["act"]:
                    f(e)

            @block.vector
            def _(e):
                for f in self.q["dve"]:
                    f(e)

            @block.gpsimd
            def _(e):
                for f in self.q["pool"]:
                    f(e)


def _consts():
    c = {}
    t = np.arange(128)
    c["ident"] = np.eye(128, dtype=np.float32)
    c["identb"] = np.eye(128, dtype=np.float32).astype(ml_dtypes.bfloat16)
    c["negib"] = (np.eye(128, dtype=np.float32) * NEG).astype(ml_dtypes.bfloat16)
    c["ones"] = np.ones((128, 128), np.float32)
    c["onesb"] = np.ones((128, 128), np.float32).astype(ml_dtypes.bfloat16)
    c["u_f"] = (t[:, None] <= t[None, :]).astype(np.float32)
    c["u_b"] = (t[:, None] >= t[None, :]).astype(np.float32)
    c["ls_f"] = (t[:, None] > t[None, :]).astype(np.float32)
    c["ls_b"] = (t[:, None] < t[None, :]).astype(np.float32)
    for n_ in ("u_f", "u_b", "ls_f", "ls_b"):
        c[n_.replace("u_", "ub_").replace("ls_", "lsb_")] = c[n_].astype(ml_dtypes.bfloat16)
    mf = (t[None, :] < t[:, None]).astype(np.float32)
    mb = (t[None, :] > t[:, None]).astype(np.float32)
    c["mi_f"] = np.tile(mf[:, None, :], (1, 4, 1)).reshape(128, 512).astype(ml_dtypes.bfloat16)
    c["mi_b"] = np.tile(mb[:, None, :], (1, 4, 1)).reshape(128, 512).astype(ml_dtypes.bfloat16)
    kc = np.arange(64)[:, None]
    qc = np.arange(64)[None, :]
    w0 = np.clip(qc - 8, 0, 48)
    colok = (kc >= w0) & (kc < w0 + 16)
    mI = np.zeros((2, 64, 16, 64), np.float32)
    mE = np.zeros((2, 64, 16, 64), np.float32)
    for krl in range(2):
        for m in range(16):
            dr = 7 - m + krl
            okI = colok & (dr >= -4) & (dr <= 3)
            okE = colok & (dr >= -7) & (dr <= 7)
            mI[krl, :, m, :] = np.where(okI, 0.0, NEG)
            mE[krl, :, m, :] = np.where(okE, 0.0, NEG)
    c["mask_i"] = mI.reshape(128, 1024).astype(ml_dtypes.bfloat16)
    c["mask_e"] = mE.reshape(128, 1024).astype(ml_dtypes.bfloat16)
    return c


def _rpb_layout(rpb):
    kc = np.arange(64)[:, None]
    qc = np.arange(64)[None, :]
    ci = np.clip(kc - qc + 15, 0, 30)
    out = np.empty((16, 2, 64, 16, 64), np.float32)
    for krl in range(2):
        for m in range(16):
            ri = int(np.clip(14 - m + krl, 0, 14))
            out[:, krl, :, m, :] = rpb[:, ri][:, ci]
    return np.ascontiguousarray(out.reshape(16, 128, 1024))


CONST_SPECS = [("ident", [128, 128], F32), ("identb", [128, 128], BF16), ("negib", [128, 128], BF16),
               ("ones", [128, 128], F32), ("onesb", [128, 128], BF16), ("u_f", [128, 128], F32), ("u_b", [128, 128], F32),
               ("ls_f", [128, 128], F32), ("ls_b", [128, 128], F32),
               ("ub_f", [128, 128], BF16), ("ub_b", [128, 128], BF16), ("lsb_f", [128, 128], BF16), ("lsb_b", [128, 128], BF16), ("mi_f", [128, 512], BF16),
               ("mi_b", [128, 512], BF16), ("mask_i", [128, 1024], BF16), ("mask_e", [128, 1024], BF16)]


class Arena:
    def __init__(self, nc, lo=16384, hi=196608):
        self.nc, self.top, self.hi = nc, lo, hi
        self.peak = lo

    def alloc(self, name, shape, dt):
        n = 1
        for d_ in shape[1:]:
            n *= d_
        nbytes = n * (4 if dt == F32 else 2)
        off = (self.top + 31) // 32 * 32
        self.top = off + nbytes
        self.peak = max(self.peak, self.top)
        assert self.top <= self.hi, "SBUF arena overflow at %s: %d" % (name, self.top)
        return self.nc.alloc_sbuf_tensor_at("s_" + name, list(shape), dt, offset=off)


def build(stage=99, dbg=(), skip=(), groups=8):
    nc = bass.Bass("TRN2", target_bir_lowering=False)
    P = Prog(nc)
    A = Arena(nc)
    din = {}
    debug = len(dbg) > 0

    def dram_in(name, shape, dt=F32):
        din[name] = nc.dram_tensor(name, list(shape), dt, kind="ExternalInput").ap()
        return din[name]

    x_d = dram_in("x", [S, D])
    ctx_d = dram_in("ctx", [CL, D])
    cvT_d = dram_in("cvT", [128, 8, 2])
    wmod_d = dram_in("w_mod", [D, 3 * D])
    bmod_d = dram_in("b_mod", [1, 3 * D])
    gpreT_d = dram_in("g_preT", [128, 8])
    gpost_d = dram_in("g_post", [1, D])
    win_d = dram_in("w_in", [D, PW])
    cwT_d = dram_in("conv_wT", [128, 32, 5])
    cbT_d = dram_in("conv_bT", [128, 32])
    cbrow_d = dram_in("conv_brow", [1, 4096])
    alog_d = dram_in("a_log", [1, 64])
    dtb_d = dram_in("dt_bias", [1, 64])
    dsk_d = dram_in("d_skip", [1, 32])
    sng_d = dram_in("ssm_norm_g", [1, 2048])
    rpb_d = dram_in("rpbT", [16, 128, 1024])
    wna_d = dram_in("w_na_out", [D, D])
    wss_d = dram_in("w_ssm_out", [2 * D, D])
    wout_d = dram_in("w_out", [D, D])
    cst_d = {n: dram_in("c_" + n, shp, dt) for (n, shp, dt) in CONST_SPECS}
    out_d = nc.dram_tensor("out", [S, D], F32, kind="ExternalOutput").ap()
    skind = "ExternalOutput" if debug else "Internal"
    ynaT_d = nc.dram_tensor("ynaT_scr", [D, S], BF16, kind=skind).ap()
    yssT_d = nc.dram_tensor("yssT_scr", [2 * D, S], BF16, kind=skind).ap()
    sg_d = nc.dram_tensor("sg_scr", [2 * D, S], F32, kind=skind).ap()
    dbg_d = {}
    win_v = win_d.rearrange("(kc p) n -> p kc n", p=128)

    psall = nc.alloc_psum_tensor("psall", [128, 4096], F32)
    psb = [Buf("ps%d" % i) for i in range(8)]

    def PS(b, c0=0, c1=512):
        return psall[:, b * 512 + c0: b * 512 + c1]

    cst = {}
    cstb = Buf("consts")
    for (n, shp, dt) in CONST_SPECS:
        if n.startswith("mask_"):
            continue
        cst[n] = A.alloc("k_" + n, shp, dt)
        P.dma("sp", "consts", cst[n][:], cst_d[n], writes=[cstb])

    def dbg_out(name, ap, shape, dt, bufs):
        if name in dbg:
            dbg_d[name] = nc.dram_tensor("dbg_" + name, list(shape), dt, kind="ExternalOutput").ap()
            P.dma("sp", "dbg_" + name, dbg_d[name], ap, reads=bufs)

    gs = A.alloc("gs", [128, 2, 8], F32)
    sh = A.alloc("sh", [128, 2, 8], F32)
    ggp = A.alloc("ggp", [128, D], F32)
    modb = Buf("mod")
    mark_h = A.top
    hT = A.alloc("hT", [128, 8, S], BF16)
    hTc = A.alloc("hTc", [128, 8, CL], BF16)
    hTb = Buf("hT")
    mark0 = A.top

    def new_phase():
        P.barrier()
        A.top = mark0

    cvT = A.alloc("cvT", [128, 8, 2], F32)
    sc = A.alloc("sc", [128, 8, 2], F32)
    lhsA = A.alloc("lhsA", [128, 8, 128], F32)
    lhsB = A.alloc("lhsB", [128, 8, 128], F32)
    bm = A.alloc("bm", [1, 3 * D], F32)
    gpreT = A.alloc("gpreT", [128, 8], F32)
    gpost_bc = A.alloc("gpost_bc", [128, D], F32)
    wm = [A.alloc("wm%d" % i, [128, 3 * D], F32) for i in range(2)]
    wmb = [Buf("wm%d" % i) for i in range(2)]
    mrow = A.alloc("mrow", [128, 2048], F32)
    mT = A.alloc("mT", [128, 32], F32)
    ab = Buf("adaln_small")
    lb = Buf("lhsAB")
    P.dma("sp", "adaln_in", cvT[:], cvT_d, writes=[ab])
    P.dma("sp", "adaln_in", bm[:], bmod_d, writes=[ab])
    P.dma("sp", "adaln_in", gpreT[:], gpreT_d, writes=[ab])
    P.dma("sp", "adaln_in", gpost_bc[:], gpost_d.partition_broadcast(128), writes=[ab])
    P.op("act", I("activation", out=sc[:], in_=cvT[:], func=AF.Silu), reads=[ab], writes=[lb])
    P.op("dve", I("tensor_copy", out=lhsA[:, :, 0:64], in_=sc[:, :, 0:1].to_broadcast([128, 8, 64])),
         reads=[lb], writes=[lb])
    P.op("dve", I("tensor_copy", out=lhsA[:, :, 64:128], in_=sc[:, :, 1:2].to_broadcast([128, 8, 64])),
         reads=[lb], writes=[lb])
    P.op("dve", I("tensor_copy", out=lhsB[:], in_=sc[:, :, 0:1].to_broadcast([128, 8, 128])),
         reads=[lb], writes=[lb])
    for k in range(8):
        s_ = k % 2
        P.dma("sp", "wm%d" % s_, wm[s_][:], wmod_d[k * 128:(k + 1) * 128, :], writes=[wmb[s_]])
        for n in range(6):
            lh = lhsA if n < 4 else lhsB
            P.mm([I("matmul", PS(n), lhsT=lh[:, k, :],
                                                              rhs=wm[s_][:, n * 512:(n + 1) * 512],
                                                              start=(k == 0), stop=False)],
                 reads=[lb, wmb[s_]], writes=[psb[n]])
    for n in range(6):
        P.mm([I("matmul", PS(n), lhsT=cst["ones"][0:1, :], rhs=bm[0:1, n * 512:(n + 1) * 512],
                                      start=False, stop=True)],
             reads=[ab, cstb], writes=[psb[n]])
    mrb = Buf("mrow")
    for n in range(4):
        P.op("act", I("activation", out=mrow[:, n * 512:(n + 1) * 512], in_=PS(n), func=AF.Copy),
             reads=[psb[n]], writes=[mrb])
    for n in range(2):
        P.op("dve", I("tensor_tensor", out=ggp[:, n * 512:(n + 1) * 512], in0=PS(4 + n),
                                                   in1=gpost_bc[:, n * 512:(n + 1) * 512], op=ALU.mult),
             reads=[psb[4 + n], ab], writes=[modb])
    fns = []
    for lc in range(2):
        p0 = 64 * lc
        for v in range(2):
            for ch in range(8):
                idx = (lc * 2 + v) * 8 + ch
                fns.append(I("matmul",
                    PS(6, idx, idx + 1), lhsT=mrow[p0:p0 + 1, v * 1024 + ch * 128: v * 1024 + (ch + 1) * 128],
                    rhs=cst["ones"][p0:p0 + 1, 0:1], start=True, stop=True))
    P.mm(fns, reads=[mrb, cstb], writes=[psb[6]])
    P.op("dve", I("tensor_copy", out=mT[:], in_=PS(6, 0, 32)), reads=[psb[6]], writes=[mrb])
    for lc in range(2):
        P.op("dve", I("tensor_copy", out=sh[:, lc, :], in_=mT[:, lc * 16: lc * 16 + 8]),
             reads=[mrb], writes=[modb])
        P.op("dve", I("scalar_tensor_tensor", out=gs[:, lc, :], in0=mT[:, lc * 16 + 8: lc * 16 + 16],
                                                            scalar=1.0, in1=gpreT[:], op0=ALU.add, op1=ALU.mult),
             reads=[mrb, ab], writes=[modb])
    dbg_out("gs", gs[:], [128, 2, 8], F32, [modb])
    dbg_out("sh", sh[:], [128, 2, 8], F32, [modb])
    dbg_out("ggp", ggp[:], [128, D], F32, [modb])

    xt = [A.alloc("xt%d" % i, [128, D], F32) for i in range(2)]
    xtb = [Buf("xt%d" % i) for i in range(2)]
    junk = A.alloc("junk", [128, D], BF16)
    junkb = Buf("junk")
    ss = [A.alloc("ss%d" % i, [128, 4], F32) for i in range(2)]
    ssb = [Buf("ss%d" % i) for i in range(2)]
    dg = [A.alloc("dg%d" % i, [128, 128], F32) for i in range(2)]
    dgb = [Buf("dg%d" % i) for i in range(2)]
    def stB_pre(it):
        s_ = it % 2
        lc = 0 if it < NT else 1
        src = x_d[it * 128:(it + 1) * 128, :] if it < NT else ctx_d[(it - NT) * 128:(it - NT + 1) * 128, :]
        P.dma("sp", "xt%d" % s_, xt[s_][:], src, writes=[xtb[s_]])
        P.op("act", I("activation", out=junk[:], in_=xt[s_][:], func=AF.Square,
                                                  accum_out=ss[s_][:, 0:1]),
             reads=[xtb[s_]], writes=[junkb, ssb[s_]])
        P.op("act", I("activation", out=ss[s_][:, 1:2], in_=ss[s_][:, 0:1], func=AF.Sqrt,
                                                  scale=1.0 / D, bias=EPS),
             reads=[ssb[s_]], writes=[ssb[s_]])
        P.op("dve", I("reciprocal", out=ss[s_][:, 2:3], in_=ss[s_][:, 1:2]),
             reads=[ssb[s_]], writes=[ssb[s_]])
        P.op("dve", I("tensor_scalar", out=dg[s_][:], in0=cst["ident"][:], scalar1=ss[s_][:, 2:3],
                                                     scalar2=None, op0=ALU.mult),
             reads=[ssb[s_], cstb], writes=[dgb[s_]])
        b0 = 2 * s_
        for hb in range(2):
            P.mm([I("matmul",
                PS(b0 + hb, j * 128, (j + 1) * 128), lhsT=xt[s_][:, (hb * 4 + j) * 128:(hb * 4 + j + 1) * 128],
                rhs=dg[s_][:], start=True, stop=True) for j in range(4)],
                reads=[xtb[s_], dgb[s_]], writes=[psb[b0 + hb]])

    def stB_post(it):
        s_ = it % 2
        lc = 0 if it < NT else 1
        b0 = 2 * s_
        for ch in range(8):
            dst = hT[:, ch, it * 128:(it + 1) * 128] if it < NT else hTc[:, ch, (it - NT) * 128:(it - NT + 1) * 128]
            P.op("act", I("activation",
                out=dst, in_=PS(b0 + ch // 4, (ch % 4) * 128, (ch % 4 + 1) * 128), func=AF.Identity,
                scale=gs[:, lc, ch:ch + 1], bias=sh[:, lc, ch:ch + 1]),
                reads=[psb[b0 + ch // 4], modb], writes=[hTb])

    stB_pre(0)
    for it in range(NTT):
        if it + 1 < NTT:
            stB_pre(it + 1)
        stB_post(it)
    dbg_out("hT", hT[:], [128, 8, S], BF16, [hTb])
    dbg_out("hTc", hTc[:], [128, 8, CL], BF16, [hTb])

    evac_rr = [0]

    def evac_copy(out_ap, in_ap, reads, writes, scale=None):
        evac_rr[0] += 1
        if scale is not None or evac_rr[0] % 2 == 0:
            if scale is None:
                P.op("act", I("activation", out=out_ap, in_=in_ap, func=AF.Copy), reads=reads, writes=writes)
            else:
                P.op("act", I("activation", out=out_ap, in_=in_ap, func=AF.Copy, scale=scale),
                     reads=reads, writes=writes)
        else:
            P.op("dve", I("tensor_copy", out=out_ap, in_=in_ap), reads=reads, writes=writes)

    def proj_fm(w_ap, wbuf, bank, tb, ctx=False):
        if ctx:
            P.mm([I("matmul", PS(bank, 0, CL), lhsT=w_ap[:, k, :], rhs=hTc[:, k, :],
                                          start=(k == 0), stop=(k == 7)) for k in range(8)],
                 reads=[wbuf, hTb], writes=[psb[bank]])
        else:
            P.mm([I("matmul", PS(bank), lhsT=w_ap[:, k, :], rhs=hT[:, k, tb * 512:(tb + 1) * 512],
                                          start=(k == 0), stop=(k == 7)) for k in range(8)],
                 reads=[wbuf, hTb], writes=[psb[bank]])

    def htile(k, T):
        return hT[:, k, T * 128:(T + 1) * 128] if T < NT else hTc[:, k, (T - NT) * 128:(T - NT + 1) * 128]

    if stage >= 2 and 'att' not in skip:
        new_phase()
        wq = [A.alloc("wq%d" % i, [128, 8, 128], BF16) for i in range(2)]
        wk = [A.alloc("wk%d" % i, [128, 8, 128], BF16) for i in range(2)]
        wv = [A.alloc("wv%d" % i, [128, 8, 128], BF16) for i in range(2)]
        wz = [A.alloc("wz%d" % i, [128, 8, 128], BF16) for i in range(2)]
        wab = [[Buf("wa%d_%d" % (j, i)) for i in range(2)] for j in range(4)]
        qT = A.alloc("qT", [128, S], BF16)
        kT = A.alloc("kT", [128, S + CL], BF16)
        vaug = A.alloc("vaug", [128, NTT, 2, 128], BF16)
        szT = A.alloc("szT", [128, S], BF16)
        ynaT = [A.alloc("ynaT%d" % i, [128, S], BF16) for i in range(2)]
        rp = A.alloc("rp", [128, 2, 1024], BF16)
        bti = A.alloc("bti", [128, 2, 1024], BF16)
        bte = A.alloc("bte", [128, 2, 1024], BF16)
        pt = [A.alloc("pt%d" % i, [128, 1024], BF16) for i in range(2)]
        rc = [A.alloc("rc%d" % i, [128, 256], F32) for i in range(2)]
        tt = [A.alloc("tt%d" % i, [128, 256], F32) for i in range(2)]
        qTb, kTb, vb, szb, rpb_, btb = Buf("qT"), Buf("kT"), Buf("vaug"), Buf("szT"), Buf("rp"), Buf("bt")
        mkb = Buf("masks")
        for n in ("mask_i", "mask_e"):
            cst[n] = A.alloc("k_" + n, [128, 1024], BF16)
            P.dma("sp", "masks", cst[n][:], cst_d[n], writes=[mkb])
        ynb = [Buf("ynaT%d" % i) for i in range(2)]
        ptb = [Buf("pt%d" % i) for i in range(2)]
        rcb = [Buf("rc%d" % i) for i in range(2)]
        P.op("pool", I("memset", vaug[:, :, 0, 64:128], 1.0), writes=[vb])
        P.op("pool", I("memset", vaug[:, :, 1, 0:64], 1.0), writes=[vb])
        cnt_s = 0
        cnt_o = 0
        for hp in range(8):
            par = hp % 2
            for j, (w, col) in enumerate(((wq, OQ), (wk, OK_), (wv, OV), (wz, OZNA))):
                P.dma("pool", "wa%d_%d" % (j, par), w[par][:], win_v[:, :, col + hp * 128: col + (hp + 1) * 128],
                      writes=[wab[j][par]])
            for hh in range(2):
                P.dma("pool", "rp", rp[:, hh, :], rpb_d[2 * hp + hh], writes=[rpb_])
            P.op("dve", I("tensor_tensor", out=bti[:], in0=rp[:],
                                                  in1=cst["mask_i"][:].unsqueeze(1).to_broadcast([128, 2, 1024]),
                                                  op=ALU.add), reads=[rpb_, mkb], writes=[btb])
            P.op("dve", I("tensor_tensor", out=bte[:], in0=rp[:],
                                                  in1=cst["mask_e"][:].unsqueeze(1).to_broadcast([128, 2, 1024]),
                                                  op=ALU.add), reads=[rpb_, mkb], writes=[btb])
            for tb in range(4):
                bank = 6 + tb % 2
                proj_fm(wq[par], wab[0][par], bank, tb)
                evac_copy(qT[:, tb * 512:(tb + 1) * 512], PS(bank), [psb[bank]], [qTb], scale=0.125)
            for tb in range(4):
                bank = 6 + tb % 2
                proj_fm(wk[par], wab[1][par], bank, tb)
                evac_copy(kT[:, tb * 512:(tb + 1) * 512], PS(bank), [psb[bank]], [kTb])
            proj_fm(wk[par], wab[1][par], 6, 0, ctx=True)
            evac_copy(kT[:, S:S + CL], PS(6, 0, CL), [psb[6]], [kTb])
            for tb in range(4):
                bank = 6 + (tb + 1) % 2
                proj_fm(wz[par], wab[3][par], bank, tb)
                P.op("act", I("activation", out=szT[:, tb * 512:(tb + 1) * 512], in_=PS(bank),
                                                                     func=AF.Silu), reads=[psb[bank]], writes=[szb])
            for t0 in range(0, NTT, 4):
                n = min(4, NTT - t0)
                bank = 6 + (t0 // 4) % 2
                for i in range(n):
                    T = t0 + i
                    P.mm([I("matmul", PS(bank, i * 128, (i + 1) * 128), lhsT=htile(k, T),
                                                                       rhs=wv[par][:, k, :], start=(k == 0), stop=(k == 7))
                          for k in range(8)], reads=[wab[2][par], hTb], writes=[psb[bank]])
                src = PS(bank, 0, n * 128).rearrange("p (t c) -> p t c", c=128)
                P.op("act", I("activation", out=vaug[:, t0:t0 + n, 0, 0:64], in_=src[:, :, 0:64],
                                                                        func=AF.Copy), reads=[psb[bank]], writes=[vb])
                P.op("dve", I("tensor_copy", out=vaug[:, t0:t0 + n, 1, 64:128],
                                                                         in_=src[:, :, 64:128]), reads=[psb[bank]], writes=[vb])
            if hp == 0:
                dbg_out("qT", qT[:], [128, S], BF16, [qTb])
                dbg_out("kT", kT[:], [128, S + CL], BF16, [kTb])
                dbg_out("vaug", vaug[:], [128, NTT, 2, 128], BF16, [vb])
                dbg_out("bti", bti[:], [128, 2, 1024], BF16, [btb])
            for hh in range(2):
                r0 = 64 * hh
                o0 = 64 * (1 - hh)
                for b in range(8):
                    if b == 0:
                        wt = [(T, bte, 7 - 2 * T) for T in range(4)]
                    elif b == 7:
                        wt = [(12 + ta, bte, 11 - 2 * ta) for ta in range(4)]
                    else:
                        wt = [(2 * b - 2 + ta, bti, 11 - 2 * ta) for ta in range(6)]
                    tiles = wt + [(16, None, 0), (17, None, 0)]
                    halves = [tiles[0:4], tiles[4:]]
                    obank = 4 + cnt_o % 2
                    op_ = cnt_o % 2
                    cnt_o += 1
                    ssets = []
                    for hi, half in enumerate(halves):
                        sset = cnt_s % 2
                        cnt_s += 1
                        ssets.append(sset)
                        sb0 = 2 * sset
                        fns = []
                        for si, (T, tab, m0) in enumerate(half):
                            oap = psall[:, sb0 * 512 + si * 256: sb0 * 512 + (si + 1) * 256]
                            fns.append(I("matmul", oap, lhsT=kT[r0:r0 + 64, T * 128:(T + 1) * 128],
                                         rhs=qT[r0:r0 + 64, b * 256:(b + 1) * 256], start=True, stop=(tab is None)))
                            if tab is not None:
                                fns.append(I("matmul", oap, lhsT=cst["identb"][:], rhs=tab[:, hh, m0 * 64:(m0 + 4) * 64],
                                             start=False, stop=True))
                        P.mm(fns, reads=[kTb, qTb, btb, cstb], writes=[psb[sb0], psb[sb0 + 1]])
                        n = len(half)
                        P.op("act", I("activation", out=pt[sset][:, 0:n * 256], in_=psall[:, sb0 * 512: sb0 * 512 + n * 256],
                                      func=AF.Exp), reads=[psb[sb0], psb[sb0 + 1]], writes=[ptb[sset]])
                    for hi, half in enumerate(halves):
                        sset = ssets[hi]
                        n = len(half)
                        fns = []
                        for si, (T, tab, m0) in enumerate(half):
                            first = (hi == 0 and si == 0)
                            last = (hi == 1 and si == n - 1)
                            fns.append(I("matmul", PS(obank, 0, 256), lhsT=vaug[:, T, hh, :],
                                         rhs=pt[sset][:, si * 256:(si + 1) * 256], start=first, stop=last))
                        P.mm(fns, reads=[vb, ptb[sset]], writes=[psb[obank]])
                    P.op("dve", I("reciprocal", out=rc[op_][o0:o0 + 64, :],
                                                                             in_=PS(obank, 0, 256)[o0:o0 + 64, :]),
                         reads=[psb[obank]], writes=[rcb[op_]])
                    P.op("dve", I("tensor_tensor",
                        out=tt[op_][r0:r0 + 64, :], in0=PS(obank, 0, 256)[r0:r0 + 64, :], in1=rc[op_][o0:o0 + 64, :],
                        op=ALU.mult), reads=[psb[obank], rcb[op_]], writes=[rcb[op_]])
                    P.op("dve", I("tensor_tensor",
                        out=ynaT[par][r0:r0 + 64, b * 256:(b + 1) * 256], in0=tt[op_][r0:r0 + 64, :],
                        in1=szT[r0:r0 + 64, b * 256:(b + 1) * 256], op=ALU.mult),
                        reads=[rcb[op_], szb], writes=[ynb[par]])
            P.dma("sp", "yna_st%d" % par, ynaT_d[hp * 128:(hp + 1) * 128, :], ynaT[par][:], reads=[ynb[par]])

    if stage >= 3 and 'gates' not in skip:
        new_phase()
        wg = [A.alloc("wg%d" % i, [128, 8, 128], BF16) for i in range(2)]
        wgb = [Buf("wg%d" % i) for i in range(2)]
        sgt = [A.alloc("sgt%d" % i, [128, 512], F32) for i in range(2)]
        sgb = [Buf("sgt%d" % i) for i in range(2)]
        ci = 0
        for fc in range(16):
            par = fc % 2
            P.dma("pool", "wg%d" % par, wg[par][:], win_v[:, :, OGNA + fc * 128: OGNA + (fc + 1) * 128], writes=[wgb[par]])
            for tb in range(4):
                bank = 6 + ci % 2
                s_ = ci % 2
                ci += 1
                proj_fm(wg[par], wgb[par], bank, tb)
                P.op("act", I("activation", out=sgt[s_][:], in_=PS(bank), func=AF.Sigmoid),
                     reads=[psb[bank]], writes=[sgb[s_]])
                P.dma("sp", "sg_st%d" % s_, sg_d[fc * 128:(fc + 1) * 128, tb * 512:(tb + 1) * 512], sgt[s_][:],
                      reads=[sgb[s_]])

    if stage >= 4:
        new_phase()
        wdt = A.alloc("wdt", [128, 8, 64], BF16)
        dtb_bc = A.alloc("dtb_bc", [128, 64], F32)
        negA = A.alloc("negA", [128, 64], F32)
        dsk_bc = A.alloc("dsk_bc", [128, 32], F32)
        cwT = A.alloc("cwT", [128, 32, 5], F32)
        cbT = A.alloc("cbT", [128, 32], F32)
        dtv = A.alloc("dtv", [128, NTT, 64], F32)
        eF = A.alloc("eF", [128, NTT, 64], F32)
        wst = A.alloc("wst", [128, NTT, 64], F32)
        etot = A.alloc("etot", [128, NTT, 64], F32)
        av_hi = A.alloc("av_hi", [128, NTT, 64], BF16)
        av_lo = A.alloc("av_lo", [128, NTT, 64], BF16)
        mark1 = A.top
        av = A.alloc("av", [128, NTT, 64], F32)
        cum = A.alloc("cum", [128, NTT, 64], F32)
        tot = A.alloc("tot", [128, NTT, 64], F32)
        smb = Buf("ssd_small")
        dqb = Buf("dtq")
        P.dma("pool", "wdt", wdt[:], win_v[:, :, ODT:ODT + 64], writes=[smb])
        P.dma("sp", "ssd_small", dtb_bc[:], dtb_d.partition_broadcast(128), writes=[smb])
        P.dma("sp", "ssd_small", negA[:], alog_d.partition_broadcast(128), writes=[smb])
        P.dma("sp", "ssd_small", dsk_bc[:], dsk_d.partition_broadcast(128), writes=[smb])
        P.dma("sp", "ssd_small", cwT[:], cwT_d, writes=[smb])
        P.dma("sp", "ssd_small", cbT[:], cbT_d, writes=[smb])
        P.op("act", I("activation", out=negA[:], in_=negA[:], func=AF.Exp), reads=[smb], writes=[smb])
        P.op("dve", I("tensor_scalar", out=negA[:], in0=negA[:], scalar1=-1.0, scalar2=None, op0=ALU.mult),
             reads=[smb], writes=[smb])
        for t0 in range(0, NTT, 8):
            n = min(8, NTT - t0)
            bank = t0 // 8
            for i in range(n):
                T = t0 + i
                P.mm([I("matmul", PS(bank, i * 64, (i + 1) * 64), lhsT=htile(k, T),
                                                                   rhs=wdt[:, k, :], start=(k == 0), stop=(k == 7))
                      for k in range(8)], reads=[smb, hTb], writes=[psb[bank]])
            P.op("dve", I("tensor_tensor",
                out=dtv[:, t0:t0 + n, :], in0=PS(bank, 0, n * 64).rearrange("p (t c) -> p t c", c=64),
                in1=dtb_bc[:].unsqueeze(1).to_broadcast([128, n, 64]), op=ALU.add),
                reads=[psb[bank], smb], writes=[dqb])
        P.op("act", I("activation", out=cum[:], in_=dtv[:], func=AF.Exp), reads=[dqb], writes=[dqb])
        P.op("act", I("activation", out=dtv[:], in_=cum[:], func=AF.Ln, bias=1.0), reads=[dqb], writes=[dqb])
        P.op("dve", I("tensor_tensor", out=av[:], in0=dtv[:], in1=negA[:].unsqueeze(1).to_broadcast([128, NTT, 64]),
                                              op=ALU.mult), reads=[dqb, smb], writes=[dqb])
        for t0 in range(0, NTT, 4):
            n = min(4, NTT - t0)
            bank = t0 // 4
            fns = []
            for i in range(n):
                T = t0 + i
                c0 = i * 128
                fns.append(I("matmul", PS(bank, c0, c0 + 32), lhsT=cst["u_f"][:],
                                                                     rhs=av[:, T, 0:32], start=True, stop=True))
                fns.append(I("matmul", PS(bank, c0 + 32, c0 + 64), lhsT=cst["u_b"][:],
                                                                     rhs=av[:, T, 32:64], start=True, stop=True))
                fns.append(I("matmul", PS(bank, c0 + 64, c0 + 128), lhsT=cst["ones"][:],
                                                                     rhs=av[:, T, :], start=True, stop=True))
            P.mm(fns, reads=[dqb, cstb], writes=[psb[bank]])
            src = PS(bank, 0, n * 128).rearrange("p (t c) -> p t c", c=128)
            P.op("dve", I("tensor_copy", out=cum[:, t0:t0 + n, :], in_=src[:, :, 0:64]),
                 reads=[psb[bank]], writes=[dqb])
            P.op("act", I("activation", out=tot[:, t0:t0 + n, :], in_=src[:, :, 64:128],
                                                                    func=AF.Copy), reads=[psb[bank]], writes=[dqb])
        P.op("act", I("activation", out=eF[:], in_=cum[:], func=AF.Exp), reads=[dqb], writes=[dqb])
        P.op("act", I("activation", out=etot[:], in_=tot[:], func=AF.Exp), reads=[dqb], writes=[dqb])
        P.op("dve", I("tensor_tensor", out=cum[:], in0=tot[:], in1=cum[:], op=ALU.subtract), reads=[dqb], writes=[dqb])
        P.op("act", I("activation", out=cum[:], in_=cum[:], func=AF.Exp), reads=[dqb], writes=[dqb])
        P.op("dve", I("tensor_tensor", out=wst[:], in0=dtv[:], in1=cum[:], op=ALU.mult), reads=[dqb], writes=[dqb])
        P.op("dve", I("tensor_copy", out=av_hi[:], in_=av[:]), reads=[dqb], writes=[dqb])
        P.op("dve", I("tensor_tensor", out=av_lo[:], in0=av[:], in1=av_hi[:], op=ALU.subtract), reads=[dqb], writes=[dqb])
        dbg_out("dtv", dtv[:], [128, NTT, 64], F32, [dqb])
        dbg_out("eF", eF[:], [128, NTT, 64], F32, [dqb])
        dbg_out("wst", wst[:], [128, NTT, 64], F32, [dqb])
        dbg_out("etot", etot[:], [128, NTT, 64], F32, [dqb])
        P.barrier()
        A.top = mark1
        wx = [A.alloc("wx0", [128, 8, 256], BF16)] * 2
        wB = [A.alloc("wB0", [128, 8, 128], BF16)] * 2
        wC = [A.alloc("wC0", [128, 8, 128], BF16)] * 2
        wzs = [A.alloc("wzs%d" % i, [128, 8, 256], BF16) for i in range(2)]
        cbr = [A.alloc("cbr%d" % i, [1, 384], BF16) for i in range(2)]
        sng_g = [A.alloc("sng_g%d" % i, [128, 256], F32) for i in range(2)]
        wgb_ = [[Buf("wg%d_%d" % (j, i)) for i in range(2)] for j in range(6)]
        for j in range(3):
            wgb_[j][1] = wgb_[j][0]
        dgc = A.alloc("dgc", [128, 4, 5, 128], BF16)
        dgcb = Buf("dgc")
        XW = 2 + S + 2 + 2 + CL + 2
        xbcT = [A.alloc("xbcT0", [128, XW], BF16)] * 2
        xbb = [Buf("xbcT0")] * 2
        xsB = A.alloc("xsB", [128, NTT, 384], BF16)
        xsBb = Buf("xsB")
        BCT = A.alloc("BCT", [128, 2, S], BF16)
        BCTb = Buf("BCT")
        Hs = A.alloc("Hs", [128, 2, NT, 256], BF16)
        Hsb = Buf("Hs")
        Hf = A.alloc("Hf", [128, 2, 256], F32)
        Hfb = [Buf("Hf0"), Buf("Hf1")]
        xw = [[A.alloc("xw%d_%d" % (d, i), [128, 256], BF16) for i in range(2)] for d in range(2)]
        xwb = [[Buf("xw%d_%d" % (d, i)) for i in range(2)] for d in range(2)]
        aU = [[A.alloc("aU%d_%d" % (d, i), [128, 8, 128], BF16) for i in range(2)] for d in range(2)]
        aUb = [[Buf("aU%d_%d" % (d, i)) for i in range(2)] for d in range(2)]
        aUb2 = [[Buf("aUlo%d_%d" % (d, i)) for i in range(2)] for d in range(2)]
        Et = [A.alloc("Et%d" % i, [128, 512], F32) for i in range(2)]
        Etb = [Buf("Et%d" % i) for i in range(2)]
        Mt = [[A.alloc("Mt%d_%d" % (d, i), [128, 4, 128], BF16) for i in range(2)] for d in range(2)]
        Mtb = [[Buf("Mt%d_%d" % (d, i)) for i in range(2)] for d in range(2)]
        xdt = [[A.alloc("xdt%d_%d" % (d, i), [128, 256], BF16) for i in range(2)] for d in range(2)]
        xdtb = [[Buf("xdt%d_%d" % (d, i)) for i in range(2)] for d in range(2)]
        xsk = [A.alloc("xsk%d" % i, [128, 256], BF16) for i in range(2)]
        xskb = [Buf("xsk%d" % i) for i in range(2)]
        szs = [A.alloc("szs%d" % i, [128, 256], F32) for i in range(2)]
        szsb = [Buf("szs%d" % i) for i in range(2)]
        t1 = [A.alloc("t1_%d" % i, [128, 256], F32) for i in range(2)]
        t2 = [A.alloc("t2_%d" % i, [128, 256], F32) for i in range(2)]
        yv = [A.alloc("yv%d" % i, [128, 256], F32) for i in range(2)]
        th, zh = t1, t2
        yn = [A.alloc("yn%d" % i, [128, 256], BF16) for i in range(2)]
        cmb = [Buf("combine%d" % i) for i in range(2)]
        ynb_ = [Buf("yn%d" % i) for i in range(2)]
        st = [A.alloc("st%d" % i, [128, 4], F32) for i in range(2)]
        stb = [Buf("st%d" % i) for i in range(2)]
        nhalf = A.alloc("nhalf", [128, 1], F32)
        P.op("pool", I("memset", nhalf[:], -0.5), writes=[smb])
        yssT = [A.alloc("yssT%d" % i, [128, 2, 128], BF16) for i in range(2)]
        yssb = [Buf("yssT%d" % i) for i in range(2)]
        P.op("pool", I("memset", xbcT[0][:], 0.0), writes=[xbb[0]])
        LAT0 = 2
        CTX0 = 2 + S + 2 + 2

        def tokbase(T):
            return LAT0 + T * 128 if T < NT else CTX0 + (T - NT) * 128

        xi = 0
        for g in range(groups):
            par = g % 2
            cids = [2 * g, 2 * g + 1, 16 + g, 24 + g]
            P.dma("pool", "wg0_%d" % par, wx[par][:], win_v[:, :, OXS + g * 256: OXS + (g + 1) * 256], writes=[wgb_[0][par]])
            P.dma("pool", "wg1_%d" % par, wB[par][:], win_v[:, :, OB + g * 128: OB + (g + 1) * 128], writes=[wgb_[1][par]])
            P.dma("pool", "wg2_%d" % par, wC[par][:], win_v[:, :, OC + g * 128: OC + (g + 1) * 128], writes=[wgb_[2][par]])
            P.dma("pool", "wg3_%d" % par, wzs[par][:], win_v[:, :, OZS + g * 256: OZS + (g + 1) * 256], writes=[wgb_[3][par]])
            P.dma("pool", "wg4_%d" % par, cbr[par][0:1, 0:256], cbrow_d[0:1, g * 256:(g + 1) * 256], writes=[wgb_[4][par]])
            P.dma("pool", "wg4_%d" % par, cbr[par][0:1, 256:384], cbrow_d[0:1, 2048 + g * 128: 2048 + (g + 1) * 128],
                  writes=[wgb_[4][par]])
            P.dma("sp", "wg5_%d" % par, sng_g[par][:], sng_d[0:1, g * 256:(g + 1) * 256].partition_broadcast(128),
                  writes=[wgb_[5][par]])
            for ci in range(4):
                for k in range(5):
                    P.op("pool", I("tensor_scalar",
                        out=dgc[:, ci, k, :], in0=cst["identb"][:], scalar1=cwT[:, cids[ci], k:k + 1], scalar2=0.0,
                        op0=ALU.mult, op1=ALU.add), reads=[cstb, smb], writes=[dgcb])
            for ci in range(4):
                w_ap = (wx[par][:, :, 0:128], wx[par][:, :, 128:256], wB[par], wC[par])[ci]
                wbuf = (wgb_[0][par], wgb_[0][par], wgb_[1][par], wgb_[2][par])[ci]
                xs_ = xi % 2
                xi += 1
                xb_ = xbcT[xs_]
                for tb in range(4):
                    bank = 6 + tb % 2
                    proj_fm(w_ap, wbuf, bank, tb)
                    evac_copy(xb_[:, LAT0 + tb * 512: LAT0 + (tb + 1) * 512], PS(bank), [psb[bank]], [xbb[xs_]])
                if ci < 3:
                    proj_fm(w_ap, wbuf, 6, 0, ctx=True)
                    evac_copy(xb_[:, CTX0: CTX0 + CL], PS(6, 0, CL), [psb[6]], [xbb[xs_]])
                if ci < 3:
                    for T in range(NTT):
                        bank = 4 + T % 2
                        base = tokbase(T)
                        fns = [I("matmul",
                            PS(bank, 0, 128), lhsT=xb_[:, base + k - 2: base + k - 2 + 128], rhs=dgc[:, ci, k, :],
                            start=(k == 0), stop=False) for k in range(5)]
                        fns.append(I("matmul", PS(bank, 0, 128), lhsT=cst["onesb"][0:1, :],
                                                                 rhs=cbr[par][0:1, ci * 128:(ci + 1) * 128],
                                                                 start=False, stop=True))
                        P.mm(fns, reads=[xbb[xs_], dgcb, cstb, wgb_[4][par]], writes=[psb[bank]])
                        P.op("act", I("activation", out=xsB[:, T, ci * 128:(ci + 1) * 128],
                                                                           in_=PS(bank, 0, 128), func=AF.Silu),
                             reads=[psb[bank]], writes=[xsBb])
                if ci >= 2:
                    for tb in range(4):
                        bank = 4 + tb % 2
                        b0_ = LAT0 + tb * 512
                        P.mm([I("matmul",
                            PS(bank), lhsT=dgc[:, ci, k, :], rhs=xb_[:, b0_ + k - 2: b0_ + k - 2 + 512],
                            start=(k == 0), stop=(k == 4)) for k in range(5)],
                            reads=[xbb[xs_], dgcb], writes=[psb[bank]])
                        P.op("act", I("activation",
                            out=BCT[:, ci - 2, tb * 512:(tb + 1) * 512], in_=PS(bank), func=AF.Silu,
                            bias=cbT[:, cids[ci]:cids[ci] + 1]), reads=[psb[bank], smb], writes=[BCTb])
            if g == 0:
                dbg_out("xsB", xsB[:], [128, NTT, 384], BF16, [xsBb])
                dbg_out("BCT", BCT[:], [128, 2, S], BF16, [BCTb])
            orders = [[16, 17] + list(range(16)), [17, 16] + list(range(15, -1, -1))]
            for d in range(2):
                P.op("pool", I("memset", Hf[:, d, :], 0.0), writes=[Hfb[d]])

            def st_pre(d, idx):
                T = orders[d][idx]
                sl = idx % 2
                c0 = d * 32 + 4 * g
                bank = 2 + 2 * d + sl
                P.op("pool", I("tensor_tensor", out=xw[d][sl][:].rearrange("p (r c) -> p r c", c=64),
                              in0=xsB[:, T, 0:256].rearrange("p (r c) -> p r c", c=64),
                              in1=wst[:, T, c0:c0 + 4].unsqueeze(2).to_broadcast([128, 4, 64]), op=ALU.mult),
                     reads=[xsBb, dqb], writes=[xwb[d][sl]])
                P.mm([I("matmul", PS(bank, 0, 256), lhsT=xsB[:, T, 256:384], rhs=xw[d][sl][:], start=True, stop=True)],
                     reads=[xsBb, xwb[d][sl]], writes=[psb[bank]])

            def st_post(d, idx):
                T = orders[d][idx]
                sl = idx % 2
                c0 = d * 32 + 4 * g
                bank = 2 + 2 * d + sl
                if T < NT:
                    P.op("act", I("activation", out=Hs[:, d, T, :], in_=Hf[:, d, :], func=AF.Copy),
                         reads=[Hfb[d]], writes=[Hsb])
                if idx == NTT - 1:
                    return
                P.op("dve", I("tensor_tensor", out=Hf[:, d, :].rearrange("p (r c) -> p r c", c=64),
                              in0=Hf[:, d, :].rearrange("p (r c) -> p r c", c=64),
                              in1=etot[:, T, c0:c0 + 4].unsqueeze(2).to_broadcast([128, 4, 64]), op=ALU.mult),
                     reads=[dqb], writes=[Hfb[d]])
                P.op("dve", I("tensor_tensor", out=Hf[:, d, :], in0=Hf[:, d, :], in1=PS(bank, 0, 256), op=ALU.add),
                     reads=[psb[bank]], writes=[Hfb[d]])

            for d in range(2):
                st_pre(d, 0)
            for idx in range(NTT):
                if idx + 1 < NTT - 1:
                    for d in range(2):
                        st_pre(d, idx + 1)
                for d in range(2):
                    st_post(d, idx)
            if g == 0:
                dbg_out("Hs", Hs[:], [128, 2, NT, 256], BF16, [Hsb])
            def out_aU(c):
                p_ = c % 2
                for d in range(2):
                    c0 = d * 32 + 4 * g
                    uu = cst["ub_f"] if d == 0 else cst["ub_b"]
                    P.op("pool", I("tensor_tensor", out=aU[d][p_][:, 0:4, :], in0=uu[:].unsqueeze(1).to_broadcast([128, 4, 128]),
                                   in1=av_hi[:, c, c0:c0 + 4].unsqueeze(2).to_broadcast([128, 4, 128]), op=ALU.mult),
                         reads=[cstb, dqb], writes=[aUb[d][p_]])
                    P.op("dve" if d == 0 else "pool",
                         I("tensor_tensor", out=aU[d][p_][:, 4:8, :], in0=uu[:].unsqueeze(1).to_broadcast([128, 4, 128]),
                           in1=av_lo[:, c, c0:c0 + 4].unsqueeze(2).to_broadcast([128, 4, 128]), op=ALU.mult),
                         reads=[cstb, dqb], writes=[aUb2[d][p_]])

            def out_front(c):
                p_ = c % 2
                bA, bB, bC = 2 + p_, 4 + p_, 6 + p_
                tsl = slice(c * 128, (c + 1) * 128)
                P.mm([I("matmul", PS(bA, 0, 128), lhsT=BCT[:, 0, tsl], rhs=BCT[:, 1, tsl], start=True, stop=True)],
                     reads=[BCTb], writes=[psb[bA]])
                P.op("pool", I("tensor_tensor", out=xsk[p_][:].rearrange("p (r c) -> p r c", c=64),
                               in0=xsB[:, c, 0:256].rearrange("p (r c) -> p r c", c=64),
                               in1=dsk_bc[:, 4 * g:4 * g + 4].unsqueeze(2).to_broadcast([128, 4, 64]), op=ALU.mult),
                     reads=[xsBb, smb], writes=[xskb[p_]])
                P.mm([I("matmul", PS(bB, 0, 256), lhsT=cst["identb"][:], rhs=xsk[p_][:], start=True, stop=False)],
                     reads=[cstb, xskb[p_]], writes=[psb[bB]])
                P.mm([I("matmul", PS(bA, 128, 384), lhsT=htile(k, c), rhs=wzs[par][:, k, :],
                        start=(k == 0), stop=(k == 7)) for k in range(8)],
                     reads=[wgb_[3][par], hTb], writes=[psb[bA]])
                P.op("act", I("activation", out=th[p_][:], in_=PS(bA, 128, 384), func=AF.Tanh, scale=0.5),
                     reads=[psb[bA]], writes=[cmb[p_]])
                P.op("act", I("activation", out=zh[p_][:], in_=PS(bA, 128, 384), func=AF.Copy, scale=0.5),
                     reads=[psb[bA]], writes=[cmb[p_]])
                P.op("dve", I("scalar_tensor_tensor", out=szs[p_][:], in0=th[p_][:], scalar=1.0, in1=zh[p_][:],
                              op0=ALU.add, op1=ALU.mult), reads=[cmb[p_]], writes=[szsb[p_]])
                for d in range(2):
                    c0 = d * 32 + 4 * g
                    uu = cst["ub_f"] if d == 0 else cst["ub_b"]
                    ls = cst["lsb_f"] if d == 0 else cst["lsb_b"]
                    mi = cst["mi_f"] if d == 0 else cst["mi_b"]
                    P.mm([I("matmul", PS(d), lhsT=ls[:], rhs=aU[d][p_][:, 0:4, :].rearrange("p r c -> p (r c)"), start=True, stop=False),
                          I("matmul", PS(d), lhsT=ls[:], rhs=aU[d][p_][:, 4:8, :].rearrange("p r c -> p (r c)"), start=False, stop=False),
                          I("matmul", PS(d), lhsT=cst["negib"][:], rhs=mi[:], start=False, stop=True)],
                         reads=[aUb[d][p_], aUb2[d][p_], cstb], writes=[psb[d]])
                    P.op("act", I("activation", out=Et[d][:], in_=PS(d), func=AF.Exp), reads=[psb[d]], writes=[Etb[d]])
                for d in range(2):
                    c0 = d * 32 + 4 * g
                    P.op("dve", I("tensor_tensor", out=Mt[d][p_][:], in0=Et[d][:].rearrange("p (r c) -> p r c", c=128),
                                  in1=PS(bA, 0, 128).unsqueeze(1).to_broadcast([128, 4, 128]), op=ALU.mult),
                         reads=[Etb[d], psb[bA]], writes=[Mtb[d][p_]])
                    P.op("pool", I("tensor_tensor", out=xdt[d][p_][:].rearrange("p (r c) -> p r c", c=64),
                                   in0=xsB[:, c, 0:256].rearrange("p (r c) -> p r c", c=64),
                                   in1=dtv[:, c, c0:c0 + 4].unsqueeze(2).to_broadcast([128, 4, 64]), op=ALU.mult),
                         reads=[xsBb, dqb], writes=[xdtb[d][p_]])

            def out_front2(c):
                p_ = c % 2
                bA, bB, bC = 2 + p_, 4 + p_, 6 + p_
                tsl = slice(c * 128, (c + 1) * 128)
                for d in range(2):
                    P.mm([I("matmul", PS(bB, r * 64, (r + 1) * 64), lhsT=Mt[d][p_][:, r, :],
                            rhs=xdt[d][p_][:, r * 64:(r + 1) * 64], start=False, stop=(d == 1 and r == 3)) for r in range(4)],
                         reads=[Mtb[d][p_], xdtb[d][p_]], writes=[psb[bB]])
                    P.mm([I("matmul", PS(bC, d * 256, (d + 1) * 256), lhsT=BCT[:, 1, tsl], rhs=Hs[:, d, c, :],
                            start=True, stop=True)], reads=[BCTb, Hsb], writes=[psb[bC]])

            def out_back(c):
                p_ = c % 2
                bA, bB, bC = 2 + p_, 4 + p_, 6 + p_
                tsl = slice(c * 128, (c + 1) * 128)
                f0 = 4 * g
                b0c = 32 + 4 * g
                P.op("dve", I("tensor_tensor", out=t1[p_][:].rearrange("p (r c) -> p r c", c=64),
                              in0=PS(bC, 0, 256).rearrange("p (r c) -> p r c", c=64),
                              in1=eF[:, c, f0:f0 + 4].unsqueeze(2).to_broadcast([128, 4, 64]), op=ALU.mult),
                     reads=[psb[bC], dqb], writes=[cmb[p_]])
                P.op("dve", I("tensor_tensor", out=t2[p_][:].rearrange("p (r c) -> p r c", c=64),
                              in0=PS(bC, 256, 512).rearrange("p (r c) -> p r c", c=64),
                              in1=eF[:, c, b0c:b0c + 4].unsqueeze(2).to_broadcast([128, 4, 64]), op=ALU.mult),
                     reads=[psb[bC], dqb, cmb[p_]], writes=[cmb[p_]])
                P.op("dve", I("tensor_tensor", out=t1[p_][:], in0=t1[p_][:], in1=t2[p_][:], op=ALU.add),
                     reads=[cmb[p_]], writes=[cmb[p_]])
                P.op("dve", I("tensor_tensor", out=yv[p_][:], in0=t1[p_][:], in1=PS(bB, 0, 256), op=ALU.add),
                     reads=[cmb[p_], psb[bB]], writes=[cmb[p_]])
                if g == 0 and c == 5:
                    dbg_out("yv", yv[p_][:], [128, 256], F32, [cmb[p_]])
                P.op("dve", I("tensor_tensor", out=yv[p_][:], in0=yv[p_][:], in1=szs[p_][:], op=ALU.mult),
                     reads=[cmb[p_], szsb[p_]], writes=[cmb[p_]])
                P.op("act", I("activation", out=t2[p_][:], in_=yv[p_][:], func=AF.Square, accum_out=st[p_][:, 0:1]),
                     reads=[cmb[p_]], writes=[cmb[p_], stb[p_]])

            def out_back_b(c):
                p_ = c % 2
                bA, bB, bC = 2 + p_, 4 + p_, 6 + p_
                tsl = slice(c * 128, (c + 1) * 128)
                P.op("pool", I("tensor_scalar", out=st[p_][:, 1:2], in0=st[p_][:, 0:1], scalar1=1.0 / 256, scalar2=EPS,
                               op0=ALU.mult, op1=ALU.add), reads=[stb[p_]], writes=[stb[p_]])
                P.op("pool", I("tensor_tensor", out=st[p_][:, 2:3], in0=st[p_][:, 1:2], in1=nhalf[:], op=ALU.pow),
                     reads=[stb[p_], smb], writes=[stb[p_]])

            def out_back_c(c):
                p_ = c % 2
                bA, bB, bC = 2 + p_, 4 + p_, 6 + p_
                tsl = slice(c * 128, (c + 1) * 128)
                P.op("dve", I("scalar_tensor_tensor", out=yn[p_][:], in0=yv[p_][:], scalar=st[p_][:, 2:3],
                              in1=sng_g[par][:], op0=ALU.mult, op1=ALU.mult),
                     reads=[cmb[p_], stb[p_], wgb_[5][par]], writes=[ynb_[p_]])
                P.mm([I("matmul", PS(bC, j * 128, (j + 1) * 128), lhsT=yn[p_][:, j * 128:(j + 1) * 128],
                        rhs=cst["identb"][:], start=True, stop=True) for j in range(2)],
                     reads=[ynb_[p_], cstb], writes=[psb[bC]])
                P.op("act", I("activation", out=yssT[p_][:], in_=PS(bC, 0, 256).rearrange("p (j c) -> p j c", c=128),
                              func=AF.Copy), reads=[psb[bC]], writes=[yssb[p_]])
                for j in range(2):
                    P.dma("sp", "yss_st%d" % p_, yssT_d[(2 * g + j) * 128:(2 * g + j + 1) * 128, tsl], yssT[p_][:, j, :],
                          reads=[yssb[p_]])

            out_aU(0)
            out_aU(1)
            out_front(0)
            for c in range(NT):
                if c >= 1:
                    out_back_b(c - 1)
                if c + 2 < NT:
                    out_aU(c + 2)
                if c + 1 < NT:
                    out_front(c + 1)
                out_front2(c)
                out_back(c)
                if c >= 1:
                    out_back_c(c - 1)
            out_back_b(NT - 1)
            out_back_c(NT - 1)


    if stage >= 5:
        new_phase()
        A.top = mark_h
        wna = A.alloc("wna", [128, 8, D], BF16)
        wss = A.alloc("wss", [128, 16, D], BF16)
        wo = A.alloc("wo", [128, 8, D], BF16)
        web = Buf("we")
        P.dma("pool", "we0", wna[:], wna_d.rearrange("(c p) n -> p c n", p=128), writes=[web])
        for hf in range(2):
            P.dma("pool", "we1", wss[:, hf * 8:(hf + 1) * 8, :],
                  wss_d[hf * 1024:(hf + 1) * 1024, :].rearrange("(c p) n -> p c n", p=128), writes=[web])
        P.dma("pool", "we2", wo[:], wout_d.rearrange("(c p) n -> p c n", p=128), writes=[web])
        ybk2 = [A.alloc("ybk%d" % i, [128, 24, 512], BF16) for i in range(2)]
        ybb2 = [Buf("ybk%d" % i) for i in range(2)]
        sgk = [A.alloc("sgk%d" % i, [128, 2, 512], F32) for i in range(2)]
        sgkb = [Buf("sgk%d" % i) for i in range(2)]
        mTt = A.alloc("mTt", [128, 8, 512], BF16)
        mTb = Buf("mTt")
        e1 = [A.alloc("e1_%d" % i, [128, 512], F32) for i in range(2)]
        e2 = [A.alloc("e2_%d" % i, [128, 512], F32) for i in range(2)]
        eb = [Buf("e%d" % i) for i in range(2)]
        xr = [A.alloc("xr%d" % i, [128, D], F32) for i in range(2)]
        xrb = [Buf("xr%d" % i) for i in range(2)]
        ot = [A.alloc("ot%d" % i, [128, D], F32) for i in range(2)]
        otb = [Buf("ot%d" % i) for i in range(2)]
        junk2 = A.alloc("junk2", [128, D], BF16)
        j2b = Buf("junk2")
        s2 = [A.alloc("s2_%d" % i, [128, 4], F32) for i in range(2)]
        s2b = [Buf("s2_%d" % i) for i in range(2)]
        ei = 0
        for tb in range(4):
            tcs = slice(tb * 512, (tb + 1) * 512)
            ybk, ybb = ybk2[tb % 2], ybb2[tb % 2]
            for tb_l in ([0, 1] if tb == 0 else ([tb + 1] if tb + 1 < 4 else [])):
                tcl = slice(tb_l * 512, (tb_l + 1) * 512)
                for cc in range(8):
                    P.dma("sp", "ybk%d" % (tb_l % 2), ybk2[tb_l % 2][:, cc, :], ynaT_d[cc * 128:(cc + 1) * 128, tcl],
                          writes=[ybb2[tb_l % 2]])
                for cc in range(16):
                    P.dma("sp", "ybk%d" % (tb_l % 2), ybk2[tb_l % 2][:, 8 + cc, :], yssT_d[cc * 128:(cc + 1) * 128, tcl],
                          writes=[ybb2[tb_l % 2]])
            for f in range(8):
                s_ = ei % 2
                ei += 1
                fsl = slice(f * 128, (f + 1) * 128)
                P.dma("act", "sgk%d" % s_, sgk[s_][:, 0, :], sg_d[f * 128:(f + 1) * 128, tcs], writes=[sgkb[s_]])
                P.dma("act", "sgk%d" % s_, sgk[s_][:, 1, :], sg_d[1024 + f * 128: 1024 + (f + 1) * 128, tcs], writes=[sgkb[s_]])
                b1 = 2 * s_
                P.mm([I("matmul", PS(b1), lhsT=wna[:, cc, fsl], rhs=ybk[:, cc, :],
                                                               start=(cc == 0), stop=(cc == 7)) for cc in range(8)],
                     reads=[web, ybb], writes=[psb[b1]])
                P.mm([I("matmul", PS(b1 + 1), lhsT=wss[:, cc, fsl], rhs=ybk[:, 8 + cc, :],
                                                               start=(cc == 0), stop=(cc == 15)) for cc in range(16)],
                     reads=[web, ybb], writes=[psb[b1 + 1]])
                P.op("dve", I("tensor_tensor", out=e1[s_][:], in0=PS(b1), in1=sgk[s_][:, 0, :], op=ALU.mult),
                     reads=[psb[b1], sgkb[s_]], writes=[eb[s_]])
                P.op("dve", I("tensor_tensor", out=e2[s_][:], in0=PS(b1 + 1), in1=sgk[s_][:, 1, :], op=ALU.mult),
                     reads=[psb[b1 + 1], sgkb[s_]], writes=[eb[s_]])
                P.op("pool", I("tensor_tensor", out=mTt[:, f, :], in0=e1[s_][:], in1=e2[s_][:], op=ALU.add),
                     reads=[eb[s_]], writes=[mTb])
            if tb == 0:
                dbg_out("mTt", mTt[:], [128, 8, 512], BF16, [mTb])
            for tl in range(4):
                T = tb * 4 + tl
                s_ = T % 2
                P.dma("act", "xr%d" % s_, xr[s_][:], x_d[T * 128:(T + 1) * 128, :], writes=[xrb[s_]])
                for hf in range(2):
                    bank = 4 + 2 * s_ + hf
                    P.mm([I("matmul",
                        PS(bank), lhsT=mTt[:, f, tl * 128:(tl + 1) * 128], rhs=wo[:, f, hf * 512:(hf + 1) * 512],
                        start=(f == 0), stop=(f == 7)) for f in range(8)],
                        reads=[mTb, web], writes=[psb[bank]])
                mer = psall[:, (4 + 2 * s_) * 512:(4 + 2 * s_) * 512 + 1024]
                P.op("act", I("activation", out=junk2[:], in_=mer, func=AF.Square,
                                                                    accum_out=s2[s_][:, 0:1]),
                     reads=[psb[4 + 2 * s_], psb[5 + 2 * s_]], writes=[j2b, s2b[s_]])
                P.op("act", I("activation", out=s2[s_][:, 1:2], in_=s2[s_][:, 0:1], func=AF.Sqrt,
                                                          scale=1.0 / D, bias=EPS), reads=[s2b[s_]], writes=[s2b[s_]])
                P.op("dve", I("reciprocal", out=s2[s_][:, 2:3], in_=s2[s_][:, 1:2]),
                     reads=[s2b[s_]], writes=[s2b[s_]])
                P.op("dve", I("scalar_tensor_tensor", out=ot[s_][:], in0=mer, scalar=s2[s_][:, 2:3],
                                                                              in1=ggp[:], op0=ALU.mult, op1=ALU.mult),
                     reads=[psb[4 + 2 * s_], psb[5 + 2 * s_], s2b[s_], modb], writes=[otb[s_]])
                P.op("pool", I("tensor_tensor", out=ot[s_][:], in0=ot[s_][:], in1=xr[s_][:], op=ALU.add),
                     reads=[xrb[s_]], writes=[otb[s_]])
                P.dma("sp", "out_st%d" % s_, out_d[T * 128:(T + 1) * 128, :], ot[s_][:], reads=[otb[s_]])

    P.barrier(["sp"])
    P.emit()
    print("sbuf peak", A.peak, "instr", {k: len(v) for k, v in P.q.items()})
    return nc, list(dbg_d.keys())


def _prep_inputs(inputs):
    f = lambda a: np.ascontiguousarray(np.asarray(a, dtype=np.float32))
    x = f(inputs["x"]); c = f(inputs["c"]); ctx = f(inputs["ctx"]); c_ctx = f(inputs["c_ctx"])
    shared = {
        "w_mod": f(inputs["w_mod"][0]),
        "b_mod": f(inputs["b_mod"][0]).reshape(1, -1),
        "g_preT": f(f(inputs["g_pre"][0]).reshape(8, 128).T),
        "g_post": f(inputs["g_post"][0]).reshape(1, -1),
        "w_in": f(inputs["w_in"][0]),
        "conv_wT": f(f(inputs["conv_w"][0]).reshape(5, 32, 128).transpose(2, 1, 0)),
        "conv_bT": f(f(inputs["conv_b"][0]).reshape(32, 128).T),
        "conv_brow": f(inputs["conv_b"][0]).reshape(1, -1),
        "a_log": f(inputs["a_log"][0]).reshape(1, 64),
        "dt_bias": f(inputs["dt_bias"][0]).reshape(1, 64),
        "d_skip": f(inputs["d_skip"][0]).reshape(1, 32),
        "ssm_norm_g": f(inputs["ssm_norm_g"][0]).reshape(1, -1),
        "rpbT": _rpb_layout(f(inputs["rpb"][0])),
        "w_na_out": f(inputs["w_na_out"][0]),
        "w_ssm_out": f(inputs["w_ssm_out"][0]),
        "w_out": f(inputs["w_out"][0]),
    }
    for k, v in _consts().items():
        shared["c_" + k] = v
    maps = []
    for b in range(x.shape[0]):
        cv = np.stack([c[b], c_ctx], axis=-1)
        m = dict(shared)
        m["x"] = x[b]
        m["ctx"] = ctx[b]
        m["cvT"] = f(cv.reshape(8, 128, 2).transpose(1, 0, 2))
        maps.append(m)
    return maps


def kernel(**inputs):
    maps = _prep_inputs(inputs)
    nc, _ = build()
    res = run_bass_kernel_spmd(nc, maps, core_ids=list(range(8)))
    return np.stack([np.asarray(r["out"], dtype=np.float32) for r in res.results], axis=0)
```
